# Optimizing a Trainium2 kernel written in Bass

```python
import math
import jax, jax.numpy as jnp
from jax import lax
import numpy as np

D_MODEL = 2048
BATCH = 8
SEQ = 2048
DEPTH = 2

N_MIXERS = 2
N_RWKV_LAYERS = (DEPTH + 1) // 2
N_DIFF_LAYERS = DEPTH // 2
NORM_EPS = 1e-6

RWKV_HEAD = 64
RWKV_HEADS = D_MODEL // RWKV_HEAD
RWKV_WIDTH = RWKV_HEADS * RWKV_HEAD
DECAY_LORA = 96
ICLR_LORA = 96
GN_EPS = 64e-5
N_SHIFT_PATHS = 6

DIFF_HEAD = 64
DIFF_HEADS = D_MODEL // (2 * DIFF_HEAD)
DIFF_WIDTH = DIFF_HEADS * 2 * DIFF_HEAD
SUBLN_EPS = 1e-5
Q_BLOCK = 128

kernel_name = "rwkv7_diffattn_interleaved_trunk"


def rms_norm(x, w, eps):
    xf = x.astype(jnp.float32)
    y = xf * lax.rsqrt(jnp.mean(xf * xf, axis=-1, keepdims=True) + eps)
    return (y * w.astype(jnp.float32)).astype(x.dtype)


def alibi_slopes(n_heads):
    return 2.0 ** (-8.0 * jnp.arange(1, n_heads + 1, dtype=jnp.float32) / n_heads)


def wkv7_step(S, inp):
    r_t, w_t, k_t, v_t, a_t, b_t = inp
    sa = jnp.einsum('bhvk,bhk->bhv', S, a_t)
    S = S * w_t[:, :, None, :] + sa[..., None] * b_t[:, :, None, :] + v_t[..., None] * k_t[:, :, None, :]
    y = jnp.einsum('bhvk,bhk->bhv', S, r_t)
    return S, y


def rwkv7_mixer(h, mu, w_in, w0, w1, w2, a0, a1, a2, k_k, k_a, r_k, ln_w, ln_b, w_out):
    B, T, D = h.shape
    H, N = RWKV_HEADS, RWKV_HEAD
    f32 = jnp.float32
    h_prev = jnp.pad(h, ((0, 0), (1, 0), (0, 0)))[:, :T]
    xs = h[None] + (h_prev - h)[None] * mu[:, None, None, :]
    x_r, x_w, x_k, x_v, x_a, x_g = xs[0], xs[1], xs[2], xs[3], xs[4], xs[5]
    proj = jnp.einsum('pbtd,dpe->pbte', jnp.stack([x_r, x_k, x_v, x_g]),
                      w_in.reshape(D, 4, RWKV_WIDTH))
    r, k, v, g_pre = proj[0], proj[1], proj[2], proj[3]
    w_log = -jax.nn.softplus(-(w0 + jnp.tanh(x_w @ w1) @ w2).astype(f32)) - 0.5
    decay = jnp.exp(-jnp.exp(w_log))
    iclr = jax.nn.sigmoid((a0 + (x_a @ a1) @ a2).astype(f32))

    def heads(z):
        return z.astype(f32).reshape(B, T, H, N)

    kk = heads(k * k_k)
    kk = kk / jnp.maximum(jnp.sqrt(jnp.sum(kk * kk, axis=-1, keepdims=True)), 1e-12)
    iclr_h = heads(iclr)
    k_h = heads(k) * (1.0 + (iclr_h - 1.0) * k_a.astype(f32).reshape(H, N))
    r_h, v_h, w_h = heads(r), heads(v), heads(decay)
    a_h = -kk
    b_h = kk * iclr_h
    seq = tuple(z.transpose(1, 0, 2, 3) for z in (r_h, w_h, k_h, v_h, a_h, b_h))
    S0 = jnp.zeros((B, H, N, N), f32)
    _, y = lax.scan(wkv7_step, S0, seq)
    y = y.transpose(1, 0, 2, 3)
    mean = jnp.mean(y, axis=-1, keepdims=True)
    var = jnp.mean(jnp.square(y - mean), axis=-1, keepdims=True)
    y = ((y - mean) * lax.rsqrt(var + GN_EPS)).reshape(B, T, D) * ln_w + ln_b
    bonus = jnp.sum(r_h * k_h * r_k.astype(f32), axis=-1, keepdims=True) * v_h
    y = y + bonus.reshape(B, T, D)
    y = (y * jax.nn.silu(g_pre.astype(f32))).astype(h.dtype)
    return y @ w_out


def diff_attn_mixer(h, w_in, lam, subln_w, w_out, lambda_init):
    B, T, D = h.shape
    H, d = DIFF_HEADS, DIFF_HEAD
    f32 = jnp.float32
    proj = h @ w_in
    q, k, v, g_pre = jnp.split(proj, 4, axis=-1)
    lam = lam.astype(f32)
    lam_full = (jnp.exp(jnp.sum(lam[0] * lam[1])) - jnp.exp(jnp.sum(lam[2] * lam[3]))
                + lambda_init)
    nb = T // Q_BLOCK
    qb = q.reshape(B, nb, Q_BLOCK, H, 2, d).transpose(1, 0, 3, 4, 2, 5)
    kt = k.reshape(B, T, H, 2, d).transpose(0, 2, 3, 1, 4)
    vt = v.reshape(B, T, H, 2 * d).transpose(0, 2, 1, 3)
    slopes = alibi_slopes(H)
    kpos = jnp.arange(T)
    scale = 1.0 / math.sqrt(d)

    def block(args):
        q_blk, start = args
        qpos = start + jnp.arange(Q_BLOCK)
        s = jnp.einsum('bhcqd,bhckd->bhcqk', q_blk, kt).astype(f32) * scale
        dist = (qpos[:, None] - kpos[None, :]).astype(f32)
        bias = jnp.where(dist >= 0, -slopes[:, None, None] * dist, -jnp.inf)
        p = jax.nn.softmax(s + bias[None, :, None], axis=-1)
        attn = p[:, :, 0] - lam_full * p[:, :, 1]
        return jnp.einsum('bhqk,bhke->bhqe', attn.astype(vt.dtype), vt)

    out = lax.map(block, (qb, jnp.arange(nb) * Q_BLOCK))
    out = out.transpose(1, 0, 3, 2, 4).reshape(B, T, H, 2 * d)
    out = rms_norm(out, subln_w, SUBLN_EPS) * (1.0 - lambda_init)
    out = out.reshape(B, T, DIFF_WIDTH) * jax.nn.silu(g_pre)
    return out @ w_out


def setup_inputs(seed: int = 0) -> dict:
    key = jax.random.key(seed)
    ks = jax.random.split(key, 24)
    nr, nd = N_RWKV_LAYERS, N_DIFF_LAYERS
    D = D_MODEL
    n = jax.random.normal
    return {
        "x": n(ks[0], (BATCH, SEQ, D), jnp.float32),
        "norm_w": 1.0 + 0.02 * n(ks[1], (DEPTH, D), jnp.float32),
        "rwkv_mu": jax.random.uniform(ks[2], (nr, N_SHIFT_PATHS, D), jnp.float32),
        "rwkv_w_in": n(ks[3], (nr, D, 4 * RWKV_WIDTH), jnp.float32) * D ** -0.5,
        "rwkv_w0": n(ks[4], (nr, D), jnp.float32) - 1.0,
        "rwkv_w1": n(ks[5], (nr, D, DECAY_LORA), jnp.float32) * D ** -0.5,
        "rwkv_w2": n(ks[6], (nr, DECAY_LORA, D), jnp.float32) * (0.1 * DECAY_LORA ** -0.5),
        "rwkv_a0": 0.1 * n(ks[7], (nr, D), jnp.float32),
        "rwkv_a1": n(ks[8], (nr, D, ICLR_LORA), jnp.float32) * D ** -0.5,
        "rwkv_a2": n(ks[9], (nr, ICLR_LORA, D), jnp.float32) * (0.1 * ICLR_LORA ** -0.5),
        "rwkv_k_k": 0.85 + 0.05 * n(ks[10], (nr, D), jnp.float32),
        "rwkv_k_a": 1.0 + 0.05 * n(ks[11], (nr, D), jnp.float32),
        "rwkv_r_k": 0.1 * n(ks[12], (nr, RWKV_HEADS, RWKV_HEAD), jnp.float32),
        "rwkv_ln_w": 1.0 + 0.02 * n(ks[13], (nr, D), jnp.float32),
        "rwkv_ln_b": 0.01 * n(ks[14], (nr, D), jnp.float32),
        "rwkv_w_out": n(ks[15], (nr, RWKV_WIDTH, D), jnp.float32) * RWKV_WIDTH ** -0.5,
        "diff_w_in": n(ks[16], (nd, D, 4 * DIFF_WIDTH), jnp.float32) * D ** -0.5,
        "diff_lambda": 0.1 * n(ks[17], (nd, 4, DIFF_HEAD), jnp.float32),
        "diff_subln_w": 1.0 + 0.02 * n(ks[18], (nd, 2 * DIFF_HEAD), jnp.float32),
        "diff_w_out": n(ks[19], (nd, DIFF_WIDTH, D), jnp.float32) * DIFF_WIDTH ** -0.5,
        "final_norm_w": 1.0 + 0.02 * n(ks[20], (D,), jnp.float32),
    }


def reference(x, norm_w, rwkv_mu, rwkv_w_in, rwkv_w0, rwkv_w1, rwkv_w2, rwkv_a0, rwkv_a1, rwkv_a2,
              rwkv_k_k, rwkv_k_a, rwkv_r_k, rwkv_ln_w, rwkv_ln_b, rwkv_w_out,
              diff_w_in, diff_lambda, diff_subln_w, diff_w_out, final_norm_w):
    for i in range(DEPTH):
        h = rms_norm(x, norm_w[i], NORM_EPS)
        j = i // N_MIXERS
        if i % N_MIXERS == 0:
            y = rwkv7_mixer(h, rwkv_mu[j], rwkv_w_in[j], rwkv_w0[j], rwkv_w1[j], rwkv_w2[j],
                            rwkv_a0[j], rwkv_a1[j], rwkv_a2[j], rwkv_k_k[j], rwkv_k_a[j], rwkv_r_k[j],
                            rwkv_ln_w[j], rwkv_ln_b[j], rwkv_w_out[j])
        else:
            lambda_init = 0.8 - 0.6 * math.exp(-0.3 * i)
            y = diff_attn_mixer(h, diff_w_in[j], diff_lambda[j], diff_subln_w[j], diff_w_out[j],
                                lambda_init)
        x = x + y
    return rms_norm(x, final_norm_w, NORM_EPS)
```

```python
import contextlib
import numpy as np
import concourse.bass as bass
import concourse.mybir as mybir
from concourse.bass_utils import run_bass_kernel_spmd

dt = mybir.dt
F32, BF16, I32 = dt.float32, dt.bfloat16, dt.int32
AF = mybir.ActivationFunctionType
ALU = mybir.AluOpType
AX = mybir.AxisListType
_DSZ = {F32: 4, BF16: 2, I32: 4, dt.uint32: 4, dt.int16: 2, dt.uint16: 2, dt.uint8: 1, dt.int8: 1,
        dt.float16: 2}


def _isap(x):
    return hasattr(x, "tensor") and hasattr(x, "ap")


class Prog:
    NQ = 8

    def __init__(self):
        nc = self.nc = bass.Bass("TRN2", target_bir_lowering=False)
        self.E = dict(pe=nc.tensor, dve=nc.vector, act=nc.scalar, pool=nc.gpsimd, sp=nc.sync)
        self.semh = {}
        for e in self.E:
            self.semh[e] = nc.alloc_semaphore("c_" + e)
        self.cnt = {e: 0 for e in self.E}
        self.stream = {e: [] for e in self.E}
        self.seen = {e: {} for e in self.E}
        self.dq = {}
        for q in ("sp", "act", "pool"):
            for i in range(self.NQ):
                self.semh[f"d_{q}_{i}"] = nc.alloc_semaphore(f"d_{q}_{i}")
            self.dq[q] = 0
        self.acc = {}
        self.nops = 0
        self._uid = 0
        self._stacks = []
        self._pemode = None
        self._perow = {}

    def sb(self, name, shape, dtype=F32):
        self._uid += 1
        name = f"{name}_{self._uid}"
        if self._stacks:
            return self._stacks[-1].enter_context(self.nc.sbuf_tensor(name, list(shape), dtype))
        return self.nc.alloc_sbuf_tensor(name, list(shape), dtype)

    @contextlib.contextmanager
    def scope(self):
        st = contextlib.ExitStack()
        self._stacks.append(st)
        try:
            yield
        finally:
            self.barrier()
            self._stacks.pop()
            st.close()

    def barrier(self):
        targets = {e: self.cnt[e] for e in self.E}
        for q, n in self.dq.items():
            for i in range(self.NQ):
                if n > i:
                    targets[f"d_{q}_{i}"] = 16 * ((n - 1 - i) // self.NQ + 1)
        for e in self.E:
            seen = self.seen[e]
            waits = []
            for k, v in targets.items():
                if v > seen.get(k, 0) and not (k == e):
                    seen[k] = v
                    waits.append((self.semh[k], v))
            self.cnt[e] += 1
            en = self.E[e]
            for sh, v in waits:
                en.wait_ge(sh, v)
            en.nop().then_inc(self.semh[e], 1)
        self.acc.clear()

    def ps(self, name, shape=(128, 512), dtype=F32):
        return self.nc.alloc_psum_tensor(name, list(shape), dtype)

    def dram(self, name, shape, dtype=F32, kind="Internal"):
        return self.nc.dram_tensor(name, list(shape), dtype, kind=kind)

    @staticmethod
    def _region(a):
        sz = _DSZ[a.dtype]
        ap = a.ap
        off = a.offset
        if str(a.space) == "DRAM":
            ext = 1
            for st, c in ap:
                ext += (c - 1) * abs(st)
            return a.tensor.name, 0, 1, off * sz, (off + ext) * sz
        pstep, pc = ap[0]
        if str(a.space) == "PSUM":
            return a.tensor.name, 0, 128, 0, 1 << 30
        if pstep == 0:
            p0, f0 = 0, off
            pstep = 1 << 60
        else:
            p0, f0 = off // pstep, off % pstep
        ext = 1
        for st, c in ap[1:]:
            ext += (c - 1) * abs(st)
        return a.tensor.name, p0, p0 + pc, f0 * sz, (f0 + ext) * sz

    def _collect(self, reg, is_write, deps):
        name, p0, p1, f0, f1 = reg
        rec = self.acc.get(name)
        if rec is None:
            return
        lists = (rec[0], rec[1]) if is_write else (rec[0],)
        for lst in lists:
            for (q0, q1, g0, g1, k, v) in lst:
                if q0 < p1 and p0 < q1 and g0 < f1 and f0 < g1:
                    if deps.get(k, 0) < v:
                        deps[k] = v

    def _record(self, reg, is_write, dep):
        name, p0, p1, f0, f1 = reg
        rec = self.acc.get(name)
        if rec is None:
            rec = self.acc[name] = ([], [])
        if is_write:
            for i in (0, 1):
                lst = rec[i]
                if lst:
                    lst[:] = [r for r in lst if not (p0 <= r[0] and r[1] <= p1 and f0 <= r[2] and r[3] <= f1)]
            rec[0].append((p0, p1, f0, f1, dep[0], dep[1]))
        else:
            lst = rec[1]
            for i, r in enumerate(lst):
                if r[4] == dep[0] and r[0] == p0 and r[1] == p1 and r[2] == f0 and r[3] == f1:
                    lst[i] = (p0, p1, f0, f1, dep[0], max(dep[1], r[5]))
                    return
            lst.append((p0, p1, f0, f1, dep[0], dep[1]))

    def op(self, eng, fn, reads=(), writes=(), dma=False):
        self.nops += 1
        rregs = [self._region(a) for a in reads if _isap(a)]
        wregs = [self._region(a) for a in writes if _isap(a)]
        pr = [r for r in rregs if r[4] == 1 << 30]
        if pr:
            rregs = [r for r in rregs if r[4] != 1 << 30]
            wregs = wregs + pr
        deps = {}
        for r in rregs:
            self._collect(r, False, deps)
        for r in wregs:
            self._collect(r, True, deps)
        if dma:
            n = self.dq[eng]
            self.dq[eng] += 1
            key = f"d_{eng}_{n % self.NQ}"
            val = 16 * (n // self.NQ + 1)
            if n >= self.NQ and deps.get(key, 0) < val - 16:
                deps[key] = val - 16
            mydep = (key, val)
            inc = 16
        else:
            self.cnt[eng] += 1
            mydep = (eng, self.cnt[eng])
            inc = 1
        waits = []
        seen = self.seen[eng]
        for k, v in deps.items():
            if k == "pe" and eng == "pe":
                continue
            if seen.get(k, 0) >= v:
                continue
            seen[k] = v
            waits.append((self.semh[k], v))
        en = self.E[eng]
        for sh, v in waits:
            en.wait_ge(sh, v)
        fn(en).then_inc(self.semh[mydep[0]], inc)
        for r in rregs:
            self._record(r, False, mydep)
        for r in wregs:
            self._record(r, True, mydep)

    def build(self):
        nc = self.nc

        return nc

    def dma(self, out, in_, q="sp", **kw):
        self.op(q, lambda e: e.dma_start(out=out, in_=in_, **kw), reads=[in_], writes=[out], dma=True)

    def _pe_mode(self, lhsT):
        def rnd(n):
            return 32 if n <= 32 else (64 if n <= 64 else 128)
        k = lhsT.ap[0][1]
        m = 1
        for st, c in lhsT.ap[1:]:
            m *= c
        mode = (rnd(k), rnd(m))
        if mode != self._pemode:
            if self._pemode is not None:
                self.E["pe"].drain()
            self._pemode = mode

    def _pe_rowgrp(self, out, lhsT):
        k = lhsT.ap[0][1]
        rg = (lhsT.offset // lhsT.ap[0][0]) if k <= 64 else -1
        name = out.tensor.name
        last = self._perow.get(name)
        if last is not None and last[0] != rg and self.seen["pe"].get("pe", 0) < last[1]:
            self.seen["pe"]["pe"] = last[1]
            self.E["pe"].wait_ge(self.semh["pe"], last[1])
        self._perow[name] = (rg, self.cnt["pe"] + 1)

    def mm(self, out, lhsT, rhs, start=True, stop=True, **kw):
        self._pe_mode(lhsT)
        self._pe_rowgrp(out, lhsT)
        self.op("pe", lambda e: e.matmul(out, lhsT, rhs, start=start, stop=stop, **kw),
                reads=[lhsT, rhs], writes=[out])

    def tr(self, out, in_, ident):
        self._pe_mode(in_)
        self._pe_rowgrp(out, in_)
        self.op("pe", lambda e: e.transpose(out, in_, ident), reads=[in_, ident], writes=[out])

    def act(self, out, in_, func, bias=0.0, scale=1.0, accum_out=None, eng="act"):
        kw = {}
        if accum_out is not None:
            kw["accum_out"] = accum_out
        self.op("act", lambda e: e.activation(out, in_, func, bias=bias, scale=scale, **kw),
                reads=[in_, bias, scale], writes=[out, accum_out])

    def tt(self, out, in0, in1, op, eng="dve"):
        self.op(eng, lambda e: e.tensor_tensor(out, in0, in1, op), reads=[in0, in1], writes=[out])

    def ts(self, out, in0, s1, op0, s2=None, op1=None, eng="dve", accum_out=None):
        kw = {}
        if op1 is not None:
            kw["op1"] = op1
        if accum_out is not None:
            kw["accum_out"] = accum_out
        self.op(eng, lambda e: e.tensor_scalar(out, in0, s1, s2, op0, **kw),
                reads=[in0, s1, s2], writes=[out, accum_out])

    def stt(self, out, in0, scalar, in1, op0, op1, eng="dve"):
        self.op(eng, lambda e: e.scalar_tensor_tensor(out, in0, scalar, in1, op0, op1),
                reads=[in0, scalar, in1], writes=[out])

    def copy(self, out, in_, eng="dve"):
        if eng == "act":
            self.op("act", lambda e: e.copy(out, in_), reads=[in_], writes=[out])
        else:
            self.op(eng, lambda e: e.tensor_copy(out, in_), reads=[in_], writes=[out])

    def memset(self, ap, val, eng="dve"):
        self.op(eng, lambda e: e.memset(ap, val), writes=[ap])

    def recip(self, out, in_):
        self.op("dve", lambda e: e.reciprocal(out, in_), reads=[in_], writes=[out])

    def reduce(self, out, in_, op=ALU.add, axis=AX.X):
        self.op("dve", lambda e: e.tensor_reduce(out, in_, axis, op), reads=[in_], writes=[out])

    def scan(self, out, d0, d1, initial, op0, op1):
        self.op("dve", lambda e: e.tensor_tensor_scan(out, d0, d1, initial, op0, op1),
                reads=[d0, d1, initial], writes=[out])

    def finish(self, out_aps):
        self.op("sp", lambda e: e.nop(), reads=list(out_aps))


import math

T = 2048
D = 2048
NCH = 16
EPS = 1e-6


def make_consts(P):
    C = {}
    ones = P.sb("c_ones", [128, 128], BF16)
    ident = P.sb("c_ident", [128, 128], BF16)
    P.memset(ones[:, :], 1.0, eng="pool")
    P.op("pool", lambda e: e.affine_select(ident[:, :], ones[:, :], [[-1, 128]], ALU.is_equal, 0.0,
                                           base=0, channel_multiplier=1),
         reads=[ones[:, :]], writes=[ident[:, :]])
    C["ones"] = ones
    C["ident"] = ident
    return C


def load_w_cast(P, wbf, w_d, col0, ncols):
    wv = w_d.rearrange("(c p) e -> p c e", p=128)
    for c4 in range(4):
        P.dma(wbf[:, c4 * 4:(c4 + 1) * 4, 0:ncols], wv[:, c4 * 4:(c4 + 1) * 4, col0:col0 + ncols], q="pool")


def phase_norm(P, C, x_d, nw_d, hT, col_off=0):
    with P.scope():
        nw = P.sb("nw", [128, 16], F32)
        P.dma(nw[:, :], nw_d.rearrange("(c p) -> p c", p=128), allow_slow_non_contiguous=True)
        xt = [P.sb(f"xt{i}", [128, D], F32) for i in range(2)]
        xn = [P.sb(f"xn{i}", [128, D], BF16) for i in range(2)]
        junk = P.sb("junk", [128, D], BF16)
        ss = [P.sb(f"ss{i}", [128, 4], F32) for i in range(2)]
        for tt in range(T // 128):
            b = tt % 2
            P.dma(xt[b][:, :], x_d[tt * 128:(tt + 1) * 128, :])
            P.act(junk[:, :], xt[b][:, :], AF.Square, accum_out=ss[b][:, 0:1])
            P.act(ss[b][:, 1:2], ss[b][:, 0:1], AF.Sqrt, bias=C["epsn"][:, 0:1], scale=1.0 / D)
            P.recip(ss[b][:, 2:3], ss[b][:, 1:2])
            P.ts(xn[b][:, :], xt[b][:, :], ss[b][:, 2:3], ALU.mult)
            for half in range(2):
                ps = P.bank[half].bitcast(BF16)
                for j in range(8):
                    c = half * 8 + j
                    P.tr(ps[:, j * 128:(j + 1) * 128], xn[b][:, c * 128:(c + 1) * 128], C["ident"][:, :])
                P.tt(hT[:, half * 8:(half + 1) * 8, col_off + tt * 128:col_off + (tt + 1) * 128],
                     ps[:, :].rearrange("p (c t) -> p c t", c=8),
                     nw[:, half * 8:(half + 1) * 8].unsqueeze(2).to_broadcast([128, 8, 128]), ALU.mult)


def proj_fm(P, wbuf, w_d, col0, rhs_fn, evac_fn, n_out_chunks=16, tbs=512):
    bi = 0
    for eg in range(n_out_chunks // 4):
        wbf = wbuf[eg % 2]
        load_w_cast(P, wbf, w_d, col0 + eg * 512, 512)
        for e4 in range(4):
            ec = eg * 4 + e4
            for tb in range(T // tbs):
                ps = P.bank[2 + bi % 4]
                bi += 1
                for c in range(NCH):
                    P.mm(ps[:, 0:tbs], wbf[:, c, e4 * 128:(e4 + 1) * 128], rhs_fn(c, tb * tbs, tbs),
                         start=(c == 0), stop=(c == NCH - 1))
                evac_fn(ec, tb, ps[:, 0:tbs])


def proj_tm(P, wbuf, w_d, col0, hT, hoff, evac_fn):
    bi = 0
    for eg in range(4):
        wbf = wbuf[eg % 2]
        load_w_cast(P, wbf, w_d, col0 + eg * 512, 512)
        for tt in range(T // 128):
            ps = P.bank[2 + bi % 4]
            bi += 1
            for c in range(NCH):
                P.mm(ps[:, :], hT[:, c, hoff + tt * 128:hoff + (tt + 1) * 128], wbf[:, c, :],
                     start=(c == 0), stop=(c == NCH - 1))
            evac_fn(tt, eg, ps[:, :])


def phase_outproj(P, C, yT_src, w_out_d, xres_d, out_d, fnw_d=None):
    with P.scope():
        wo = P.sb("wo", [128, 16, D], BF16)
        wv = w_out_d.rearrange("(c p) e -> p c e", p=128)
        for c in range(16):
            P.dma(wo[:, c, :], wv[:, c, :], q="pool")
        yT = P.sb("yTo", [128, 16, T], BF16)
        yv = yT_src.rearrange("(c p) t -> p c t", p=128)
        for c in range(16):
            P.dma(yT[:, c, :], yv[:, c, :])
        if fnw_d is not None:
            fnw = P.sb("fnw", [128, D], F32)
            P.dma(fnw[:, :], fnw_d.rearrange("(o d) -> o d", o=1).partition_broadcast(128).rearrange("p o d -> p (o d)"))
        xr = [P.sb(f"xr{i}", [128, D], F32) for i in range(2)]
        ot = [P.sb(f"ot{i}", [128, D], F32) for i in range(2)]
        junk = P.sb("junk2", [128, D], BF16)
        ss = [P.sb(f"ss2{i}", [128, 4], F32) for i in range(2)]
        bi = 0
        for tt in range(T // 128):
            b = tt % 2
            P.dma(xr[b][:, :], xres_d[tt * 128:(tt + 1) * 128, :])
            for eg in range(4):
                ps = P.bank[bi % 4]
                bi += 1
                for c in range(NCH):
                    P.mm(ps[:, :], yT[:, c, tt * 128:(tt + 1) * 128], wo[:, c, eg * 512:(eg + 1) * 512],
                         start=(c == 0), stop=(c == NCH - 1))
                P.tt(xr[b][:, eg * 512:(eg + 1) * 512], ps[:, :], xr[b][:, eg * 512:(eg + 1) * 512], ALU.add)
            if fnw_d is None:
                P.dma(out_d[tt * 128:(tt + 1) * 128, :], xr[b][:, :])
            else:
                P.act(junk[:, :], xr[b][:, :], AF.Square, accum_out=ss[b][:, 0:1])
                P.act(ss[b][:, 1:2], ss[b][:, 0:1], AF.Sqrt, bias=C["epsn"][:, 0:1], scale=1.0 / D)
                P.recip(ss[b][:, 2:3], ss[b][:, 1:2])
                P.stt(ot[b][:, :], xr[b][:, :], ss[b][:, 2:3], fnw[:, :], ALU.mult, ALU.mult, eng="pool" if False else "dve")
                P.dma(out_d[tt * 128:(tt + 1) * 128, :], ot[b][:, :])


def layer1(P, C, x1_d, nw_d, w_in_d, lam_d, subln_d, w_out_d, fnw_d, out_d, lambda_init, S):
    H = 16
    scale = 1.0 / 8.0
    with P.scope():
        hT = P.sb("hT1", [128, 16, T], BF16)
        phase_norm(P, C, x1_d, nw_d, hT)
        wbuf = [P.sb(f"wbuf{i}", [128, 16, 512], BF16) for i in range(2)]
        ost = [P.sb(f"ost{i}", [128, 512], BF16) for i in range(4)]
        cnt = [0]

        def mk_evac_fm(dst):
            def f(ec, tb, ps):
                o = ost[cnt[0] % 4]
                if cnt[0] % 2 == 0:
                    P.copy(o[:, :], ps, eng="act")
                else:
                    P.copy(o[:, :], ps, eng="dve")
                cnt[0] += 1
                P.dma(dst[ec * 128:(ec + 1) * 128, tb * 512:(tb + 1) * 512], o[:, :])
            return f

        def mk_evac_tm(dst, silu):
            def f(tt, eg, ps):
                o = ost[cnt[0] % 4]
                if silu:
                    P.act(o[:, :], ps, AF.Silu)
                elif cnt[0] % 2 == 0:
                    P.copy(o[:, :], ps, eng="act")
                else:
                    P.copy(o[:, :], ps, eng="dve")
                cnt[0] += 1
                P.dma(dst[tt * 128:(tt + 1) * 128, eg * 512:(eg + 1) * 512], o[:, :])
            return f

        rhs_fn = lambda c, t0, n: hT[:, c, t0:t0 + n]
        proj_fm(P, wbuf, w_in_d, 0 * D, rhs_fn, mk_evac_fm(S["q"]))
        proj_fm(P, wbuf, w_in_d, 1 * D, rhs_fn, mk_evac_fm(S["k"]))
        proj_tm(P, wbuf, w_in_d, 2 * D, hT, 0, mk_evac_tm(S["v"], False))
        proj_tm(P, wbuf, w_in_d, 3 * D, hT, 0, mk_evac_tm(S["g"], True))

    with P.scope():
        lam = P.sb("lam", [128, 4, 64], F32)
        P.dma(lam[:, :, :].rearrange("p a b -> p (a b)"),
              lam_d.rearrange("(o a) b -> o (a b)", o=1).partition_broadcast(128).rearrange("p o f -> p (o f)"))
        lt = P.sb("lt", [128, 8], F32)
        lj = P.sb("lj", [128, 64], F32)
        for i in range(2):
            P.tt(lj[:, :], lam[:, 2 * i, :], lam[:, 2 * i + 1, :], ALU.mult)
            P.reduce(lt[:, i:i + 1], lj[:, :])
            P.act(lt[:, 2 + i:3 + i], lt[:, i:i + 1], AF.Exp)
        P.tt(lt[:, 4:5], lt[:, 3:4], lt[:, 2:3], ALU.subtract)
        P.ts(lt[:, 5:6], lt[:, 4:5], -lambda_init, ALU.add)
        nlam = lt[:, 5:6]
        slw = P.sb("slw", [128, 128], F32)
        P.dma(slw[:, :], subln_d.rearrange("(o d) -> o d", o=1).partition_broadcast(128).rearrange("p o d -> p (o d)"))
        P.ts(slw[:, :], slw[:, :], 1.0 - lambda_init, ALU.mult)
        kpos_i = P.sb("kpos_i", [128, 16], I32)
        kpos = P.sb("kpos", [128, 16], F32)
        P.op("pool", lambda e: e.iota(kpos_i[:, :], [[128, 16]], base=0, channel_multiplier=1), writes=[kpos_i[:, :]])
        P.copy(kpos[:, :], kpos_i[:, :], eng="dve")
        qpos_i = P.sb("qpos_i", [65, T], I32)
        qpos = P.sb("qpos", [65, T], F32)
        P.op("pool", lambda e: e.iota(qpos_i[64:65, :], [[1, T]], base=0, channel_multiplier=0), writes=[qpos_i[64:65, :]])
        P.copy(qpos[64:65, :], qpos_i[64:65, :], eng="dve")
        bkh = [P.sb(f"bkh{i}", [128, 16], F32) for i in range(2)]
        st4 = P.sb("st4", [128, 32], F32)
        cbig = P.sb("cbig", [128, 128], BF16)
        cmask = P.sb("cmask", [128, 128], BF16)
        P.memset(cbig[:, :], 3.0e38, eng="pool")
        P.op("pool", lambda e: e.affine_select(cmask[:, :], cbig[:, :], [[1, 128]], ALU.is_ge, 0.0,
                                               base=0, channel_multiplier=-1),
             reads=[cbig[:, :]], writes=[cmask[:, :]])
        qa = [[P.sb(f"qa{b}{c}", [65, T], BF16) for c in range(2)] for b in range(2)]
        ka = [[P.sb(f"ka{b}{c}", [65, T], BF16) for c in range(2)] for b in range(2)]
        for b in range(2):
            for c in range(2):
                P.memset(ka[b][c][64:65, :], 1.0, eng="dve")
        va = [P.sb(f"va{b}", [128, 16, 130], BF16) for b in range(2)]
        for b in range(2):
            P.memset(va[b][:, :, 128:130], 1.0, eng="dve")
        gt = [P.sb(f"gt{b}", [128, 16, 128], BF16) for b in range(2)]
        pt = [P.sb(f"pt{i}", [128, 512], BF16) for i in range(3)]
        yT = P.sb("yT", [128, T], BF16)
        yTb = [P.sb(f"yTb{b}", [128, T], BF16) for b in range(2)]
        o0 = P.sb("o0", [128, 128], F32)
        o1 = P.sb("o1", [128, 128], F32)
        at = P.sb("at", [128, 128], F32)
        yb = P.sb("yb", [128, 128], BF16)
        junk = P.sb("junk3", [128, 128], F32)
        sbank = [P.bank[0], P.bank[1], P.bank[2]]
        accb = [[P.bank[3], P.bank[4]], [P.bank[5], P.bank[6]]]
        trb = P.bank[7].bitcast(BF16)
        at4 = [P.sb(f"at4{i}", [128, 128], F32) for i in range(4)]

        def head_setup(h):
            b = h % 2
            slope = 2.0 ** (-8.0 * (h + 1) / H)
            for c in range(2):
                P.dma(qa[b][c][0:64, :], S["q"][h * 128 + c * 64:h * 128 + (c + 1) * 64, :])
                P.dma(ka[b][c][0:64, :], S["k"][h * 128 + c * 64:h * 128 + (c + 1) * 64, :])
                P.ts(qa[b][c][64:65, :], qpos[64:65, :], -slope / scale, ALU.mult)
            vsrc = S["v"][:, h * 128:(h + 1) * 128].rearrange("(kb p) e -> p kb e", p=128)
            gsrc = S["g"][:, h * 128:(h + 1) * 128].rearrange("(kb p) e -> p kb e", p=128)
            for k4 in range(4):
                P.dma(va[b][:, k4 * 4:(k4 + 1) * 4, 0:128], vsrc[:, k4 * 4:(k4 + 1) * 4, :])
                P.dma(gt[b][:, k4 * 4:(k4 + 1) * 4, :], gsrc[:, k4 * 4:(k4 + 1) * 4, :])
            P.ts(bkh[b][:, :], kpos[:, :], slope, ALU.mult)

        def stage1(it, slot):
            h, qg, c, kb = it
            b = h % 2
            if qg == 0 and c == 0 and kb == 0:
                head_setup(h)
            q0 = max(kb, 4 * qg) * 128
            q1 = (4 * qg + 4) * 128
            n = q1 - q0
            sb_ = sbank[slot]
            p_ = pt[slot]
            P.mm(sb_[:, 0:n], ka[b][c][0:65, kb * 128:(kb + 1) * 128], qa[b][c][0:65, q0:q1])
            P.act(p_[:, 0:n], sb_[:, 0:n], AF.Exp, bias=bkh[b][:, kb:kb + 1], scale=scale)
            if kb >= 4 * qg:
                P.tt(p_[:, 0:128], p_[:, 0:128], cmask[:, :], ALU.min)

        def stage2(it, slot):
            h, qg, c, kb = it
            b = h % 2
            q0 = max(kb, 4 * qg) * 128
            p_ = pt[slot]
            for qb in range(max(kb, 4 * qg), 4 * qg + 4):
                qq = qb - 4 * qg
                col = qb * 128 - q0
                acc = accb[c][qq // 2][:, (qq % 2) * 256:(qq % 2) * 256 + 130]
                P.mm(acc, p_[:, col:col + 128], va[b][:, kb, :], start=(kb == 0 and qq % 2 == 0), stop=(kb == qb))
            if c == 1 and kb == 4 * qg + 3:
                finalize_group(h, qg)

        def finalize_group(h, qg):
            b = h % 2
            for qq in range(4):
                a0 = accb[0][qq // 2][:, (qq % 2) * 256:(qq % 2) * 256 + 130]
                a1 = accb[1][qq // 2][:, (qq % 2) * 256:(qq % 2) * 256 + 130]
                s_ = st4[:, qq * 8:(qq + 1) * 8]
                P.recip(s_[:, 0:1], a0[:, 128:129])
                P.recip(s_[:, 1:2], a1[:, 128:129])
                P.tt(s_[:, 2:3], s_[:, 1:2], nlam, ALU.mult)
                P.ts(o0[:, :], a0[:, 0:128], s_[:, 0:1], ALU.mult)
                P.stt(at4[qq][:, :], a1[:, 0:128], s_[:, 2:3], o0[:, :], ALU.mult, ALU.add)
            for qq in range(4):
                qb = 4 * qg + qq
                s_ = st4[:, qq * 8:(qq + 1) * 8]
                at = at4[qq]
                P.tt(junk[:, :], at[:, :], at[:, :], ALU.mult, eng="pool")
                P.reduce(s_[:, 3:4], junk[:, :])
                P.act(s_[:, 4:5], s_[:, 3:4], AF.Ln, bias=C["epss"][:, 0:1], scale=1.0 / 128)
                P.act(s_[:, 5:6], s_[:, 4:5], AF.Exp, scale=-0.5)
                P.stt(o1[:, :], at[:, :], s_[:, 5:6], slw[:, :], ALU.mult, ALU.mult)
                P.tt(yb[:, :], o1[:, :], gt[b][:, qb, :], ALU.mult)
                P.tr(trb[:, qq * 128:(qq + 1) * 128], yb[:, :], C["ident"][:, :])
            P.copy(yTb[b][:, qg * 512:(qg + 1) * 512], trb[:, 0:512], eng="dve")
            if qg == 3:
                P.dma(S["y"][h * 128:(h + 1) * 128, :], yTb[b][:, :])

        items = [(h, qg, c, kb) for h in range(H) for qg in range(4) for c in range(2) for kb in range(4 * qg + 4)]
        prev = None
        for idx, it in enumerate(items):
            stage1(it, idx % 3)
            if prev is not None:
                stage2(*prev)
            prev = (it, idx % 3)
        stage2(*prev)
    phase_outproj(P, C, S["y"], w_out_d, x1_d, out_d, fnw_d=fnw_d)


def setup_prog():
    P = Prog()
    P.bank = [P.ps(f"bank{i}") for i in range(8)]
    C = make_consts(P)
    epsn = P.sb("epsn", [128, 1], F32)
    P.memset(epsn[:, :], EPS)
    epss = P.sb("epss", [128, 1], F32)
    P.memset(epss[:, :], 1e-5)
    C["epsn"] = epsn
    C["epss"] = epss
    return P, C


C0 = math.exp(-0.5)
GN_EPS = 64e-5
TB = 256
CB = 4
G8 = 8
DBG = {}


def layer0(P, C, x_d, nw_d, mu_d, w_in_d, w0_d, w1_d, w2_d, a0_d, a1_d, a2_d, kk_d, ka_d, rk_d,
           lnw_d, lnb_d, w_out_d, x1_d, S):
    with P.scope():
        xw1T = P.sb("xw1T", [96, T], BF16)
        xa1T = P.sb("xa1T", [96, T], BF16)
        with P.scope():
            hT = P.sb("hT0", [128, 16, T + 1], BF16)
            P.memset(hT[:, :, 0:1], 0.0)
            phase_norm(P, C, x_d, nw_d, hT, col_off=1)
            mu = P.sb("mu", [128, 6, 16], F32)
            om = P.sb("om", [128, 6, 16], F32)
            for s in range(6):
                P.dma(mu[:, s, :], mu_d[s, :].rearrange("(c p) -> p c", p=128), allow_slow_non_contiguous=True)
            P.ts(om[:, :, :], mu[:, :, :], -1.0, ALU.mult, 1.0, ALU.add)
            xp = P.sb("xp", [128, 16, T], BF16)
            tmp = [P.sb(f"xtmp{i}", [128, T], BF16) for i in range(2)]
            wbuf = [P.sb(f"wbuf0{i}", [128, 16, 512], BF16) for i in range(2)]
            ost = [P.sb(f"ost0{i}", [128, 512], F32) for i in range(4)]
            w1b = P.sb("w1b", [128, 16, 96], BF16)
            a1b = P.sb("a1b", [128, 16, 96], BF16)
            P.dma(w1b[:, :, :], w1_d.rearrange("(c p) r -> p c r", p=128), q="pool")
            P.dma(a1b[:, :, :], a1_d.rearrange("(c p) r -> p c r", p=128), q="pool")
            cnt = [0]

            def make_xp(path):
                for c in range(16):
                    tb_ = tmp[c % 2]
                    P.act(tb_[:, :], hT[:, c, 1:T + 1], AF.Copy, scale=om[:, path, c:c + 1])
                    P.stt(xp[:, c, :], hT[:, c, 0:T], mu[:, path, c:c + 1], tb_[:, :], ALU.mult, ALU.add)

            def mk_evac(dst):
                def f(ec, tb, ps):
                    o = ost[cnt[0] % 4]
                    if cnt[0] % 2 == 0:
                        P.copy(o[:, :], ps, eng="act")
                    else:
                        P.copy(o[:, :], ps, eng="dve")
                    cnt[0] += 1
                    P.dma(dst[ec * 128:(ec + 1) * 128, tb * 512:(tb + 1) * 512], o[:, :])
                return f

            rhs_fn = lambda c, t0, n: xp[:, c, t0:t0 + n]
            for path, pidx, key in ((0, 0, "r"), (2, 1, "k"), (3, 2, "v"), (5, 3, "g"))[:DBG.get("nproj", 4)]:
                make_xp(path)
                proj_fm(P, wbuf, w_in_d, pidx * D, rhs_fn, mk_evac(S[key]))
            for path, wl, dst, fn in ((1, w1b, xw1T, AF.Tanh), (4, a1b, xa1T, AF.Copy)):
                make_xp(path)
                for tb in range(4):
                    ps = P.bank[2 + tb]
                    for c in range(NCH):
                        P.mm(ps[0:96, :], wl[:, c, :], xp[:, c, tb * 512:(tb + 1) * 512],
                             start=(c == 0), stop=(c == NCH - 1))
                    P.act(dst[:, tb * 512:(tb + 1) * 512], ps[0:96, :], fn)

        with P.scope():
            def pvec(name, src):
                t_ = P.sb(name, [128, 16], F32)
                P.dma(t_[:, :], src.rearrange("(c p) -> p c", p=128), allow_slow_non_contiguous=True)
                return t_
            w0 = pvec("w0", w0_d)
            a0 = pvec("a0", a0_d)
            k_k = pvec("k_k", kk_d)
            k_a = pvec("k_a", ka_d)
            r_k = pvec("r_k", rk_d)
            ln_w = pvec("ln_w", lnw_d)
            ln_b = pvec("ln_b", lnb_d)
            w2b = P.sb("w2b", [96, D], BF16)
            a2b = P.sb("a2b", [96, D], BF16)
            P.dma(w2b[:, :], w2_d, q="pool")
            P.dma(a2b[:, :], a2_d, q="pool")
            ones32 = P.sb("ones32", [128, 128], F32)
            P.memset(ones32[:, :], 1.0)
            suui = P.sb("suui", [128, 2, 64], F32)
            slm = P.sb("slm", [128, 64], F32)
            i64 = P.sb("i64", [128, 64], F32)
            for par in range(2):
                ps_ = slice(par * 64, (par + 1) * 64)
                for (dst, pat, cm_, cmp_) in ((suui[ps_, 0, :], [[1, 64]], -1, ALU.is_gt),
                                              (suui[ps_, 1, :], [[1, 64]], -1, ALU.is_ge),
                                              (slm[ps_, :], [[-1, 64]], 1, ALU.is_gt),
                                              (i64[ps_, :], [[-1, 64]], 1, ALU.is_equal)):
                    P.op("pool", lambda e, dst=dst, pat=pat, cm_=cm_, cmp_=cmp_, ps_=ps_:
                         e.affine_select(dst, ones32[ps_, 0:64], pat, cmp_, 0.0, base=0, channel_multiplier=cm_),
                         reads=[ones32[ps_, 0:64]], writes=[dst])
            bones = P.sb("bones", [128, 128], BF16)
            bo32 = P.sb("bo32", [128, 128], F32)
            P.memset(bones[:, :], 0.0)
            P.memset(bo32[:, :], 0.0)
            for par in range(2):
                ps_ = slice(par * 64, (par + 1) * 64)
                P.memset(bones[ps_, ps_], 1.0)
                P.memset(bo32[ps_, ps_], 1.0 / 64)
            cm = P.sb("cm", [128, TB], F32)
            P.memset(cm[:, :], 1.0)
            P.memset(cm[:, :].rearrange("p (c t) -> p c t", t=64)[:, :, 0:1], 0.0)
            gne = P.sb("gne", [128, 1], F32)
            P.memset(gne[:, :], GN_EPS)
            ident = C["ident"]
            Zf = P.sb("Zf", [128, 16, 64], F32)
            Zb = P.sb("Zb", [128, 16, 64], BF16)
            P.memset(Zf[:, :, :], 0.0)
            P.memset(Zb[:, :, :], 0.0)
            AR = P.sb("AR", [128, G8, 2, TB], BF16)
            Bt = P.sb("Bt", [128, G8, TB], BF16)
            Kt = P.sb("Kt", [128, G8, TB], BF16)
            TM = P.sb("TM", [128, G8, CB, 3, 64], BF16)
            NTM = P.sb("NTM", [128, G8, CB, 2, 64], BF16)
            LM = P.sb("LM", [128, G8, CB, 2, 64], BF16)
            TTf = P.sb("TTf", [128, G8, CB, 64], F32)
            TTb = P.sb("TTb", [128, G8, CB, 64], BF16)
            Nm = [P.sb(f"Nm{i}", [128, G8, CB, 64], BF16) for i in range(2)]
            NTm = [P.sb(f"NTm{i}", [128, G8, CB, 64], BF16) for i in range(2)]
            bonus = P.sb("bonus", [128, G8, TB], F32)
            gate = P.sb("gate", [128, G8, TB], BF16)
            Ysb = P.sb("Ysb", [128, G8, TB], F32)
            gC = P.sb("gC", [128, G8, CB], F32)
            Xb = P.sb("Xb", [128, G8, 64], BF16)
            Ub = P.sb("Ub", [128, G8, 64], BF16)
            names32 = ["r32", "k32", "v32", "g32", "sg", "ic", "css", "gam", "gin", "d1", "d2", "kk", "lnq", "bb", "u"]
            names16 = ["sq", "bg", "kg", "vb", "rk", "yo"]
            alias = {"gpr": "d1", "gen": "d2", "rn": "lnq", "kkn": "kk", "km": "u",
                     "msq": "sg", "var": "ic", "sd": "css", "rstd": "gam", "dd": "gin", "yn": "bb", "y2": "d2", "mean": "d1"}
            NSET = 4
            tmps = []
            for i in range(NSET):
                d_ = {n: P.sb(f"{n}{i}", [128, TB], F32) for n in names32}
                d_.update({n: P.sb(f"{n}{i}", [128, TB], BF16) for n in names16})
                for k_, v_ in alias.items():
                    d_[k_] = d_[v_]
                tmps.append(d_)

            def interleave(gens):
                gens = list(gens)
                while gens:
                    for g_ in list(gens):
                        try:
                            next(g_)
                        except StopIteration:
                            gens.remove(g_)

            def c3(ap):
                return ap.rearrange("p (c t) -> p c t", t=64)

            def prep(ec, blk):
                e8 = ec % G8
                t0 = blk * TB
                W = tmps[ec % NSET]
                col = slice(ec, ec + 1)
                rows = slice(ec * 128, (ec + 1) * 128)
                pb = ec % 2
                bA = P.bank[0 + pb]
                bB = P.bank[2 + pb]
                bG = P.bank[2 + pb]
                bT = P.bank[4 + pb].bitcast(BF16)
                bM = P.bank[6 + pb]
                for n, key in (("r32", "r"), ("k32", "k"), ("v32", "v"), ("g32", "g")):
                    P.dma(W[n][:, :], S[key][rows, t0:t0 + TB])
                yield
                P.mm(bA[:, 0:TB], w2b[:, rows], xw1T[:, t0:t0 + TB])
                P.mm(bA[:, TB:2 * TB], a2b[:, rows], xa1T[:, t0:t0 + TB])
                P.act(W["sg"][:, :], bA[:, 0:TB], AF.Sigmoid, bias=w0[:, col])
                P.act(W["ic"][:, :], bA[:, TB:2 * TB], AF.Sigmoid, bias=a0[:, col])
                P.act(gate[:, e8, :], W["g32"][:, :], AF.Silu)
                yield
                P.scan(W["css"][:, :], cm[:, :], W["sg"][:, :], 0.0, ALU.mult, ALU.add)
                P.tt(W["d1"][:, :], W["css"][:, :], W["sg"][:, :], ALU.subtract, eng="pool")
                P.tt(c3(W["d2"][:, :]), c3(W["css"][:, :])[:, :, 63:64].to_broadcast([128, CB, 64]),
                     c3(W["css"][:, :]), ALU.subtract)
                P.ts(W["kk"][:, :], W["k32"][:, :], k_k[:, col], ALU.mult)
                P.tt(W["sq"][:, :], W["kk"][:, :], W["kk"][:, :], ALU.mult, eng="pool")
                yield
                P.act(W["gam"][:, :], W["css"][:, :], AF.Exp, scale=-C0)
                P.act(W["gin"][:, :], W["css"][:, :], AF.Exp, scale=C0)
                P.act(W["gpr"][:, :], W["d1"][:, :], AF.Exp, scale=-C0)
                P.act(W["gen"][:, :], W["d2"][:, :], AF.Exp, scale=-C0)
                P.copy(gC[:, e8, :], c3(W["gam"][:, :])[:, :, 63], eng="pool")
                P.mm(bB[:, 0:TB], bones[:, :], W["sq"][:, :])
                P.act(W["lnq"][:, :], bB[:, 0:TB], AF.Ln)
                P.act(W["rn"][:, :], W["lnq"][:, :], AF.Exp, scale=-0.5)
                yield
                P.tt(W["kkn"][:, :], W["kk"][:, :], W["rn"][:, :], ALU.mult)
                P.stt(AR[:, e8, 0, :], W["kkn"][:, :], -1.0, W["gpr"][:, :], ALU.mult, ALU.mult)
                P.tt(W["bb"][:, :], W["kkn"][:, :], W["ic"][:, :], ALU.mult)
                P.tt(Bt[:, e8, :], W["bb"][:, :], W["gin"][:, :], ALU.mult, eng="pool")
                P.tt(W["bg"][:, :], W["bb"][:, :], W["gen"][:, :], ALU.mult)
                P.ts(W["u"][:, :], W["ic"][:, :], -1.0, ALU.add, k_a[:, col], ALU.mult)
                P.stt(W["km"][:, :], W["u"][:, :], 1.0, W["k32"][:, :], ALU.add, ALU.mult)
                P.tt(Kt[:, e8, :], W["km"][:, :], W["gin"][:, :], ALU.mult, eng="pool")
                P.tt(W["kg"][:, :], W["km"][:, :], W["gen"][:, :], ALU.mult, eng="pool")
                P.tt(AR[:, e8, 1, :], W["r32"][:, :], W["gam"][:, :], ALU.mult)
                P.stt(W["rk"][:, :], W["r32"][:, :], r_k[:, col], W["km"][:, :], ALU.mult, ALU.mult)
                P.copy(W["vb"][:, :], W["v32"][:, :], eng="act")
                yield
                P.mm(bG[:, TB:2 * TB], bones[:, :], W["rk"][:, :])
                P.tt(bonus[:, e8, :], bG[:, TB:2 * TB], W["v32"][:, :], ALU.mult)
                for par in range(2):
                    for ch in range(CB):
                        ps_ = slice(par * 64, (par + 1) * 64)
                        for j, nm in enumerate(("bg", "kg", "vb")):
                            o_ = (ch * 3 + j) * 64
                            P.tr(bT[ps_, o_:o_ + 64], W[nm][ps_, ch * 64:(ch + 1) * 64], ident[ps_, ps_])
                P.copy(TM[:, e8, :, :, :].rearrange("p c j t -> p (c j t)"), bT[:, 0:CB * 3 * 64], eng="act")
                yield
                mk3 = suui[:, :, :].rearrange("p a t -> p (a t)").unsqueeze(1).to_broadcast([128, CB, 128])
                for par in range(2):
                    for ch in range(CB):
                        cs_ = slice(ch * 64, (ch + 1) * 64)
                        ps_ = slice(par * 64, (par + 1) * 64)
                        P.mm(bM[ps_, ch * 128:(ch + 1) * 128], Bt[ps_, e8, cs_], AR[ps_, e8, :, cs_])
                P.tt(NTM[:, e8, :, :, :].rearrange("p c a t -> p c (a t)"),
                     bM[:, :].rearrange("p (c x) -> p c x", x=128), mk3, ALU.mult)
                P.copy(NTm[0][:, e8, :, :], NTM[:, e8, :, 0, :], eng="pool")
                P.tt(TTf[:, e8, :, :], NTM[:, e8, :, 0, :], i64[:, :].unsqueeze(1).to_broadcast([128, CB, 64]), ALU.add)
                P.copy(TTb[:, e8, :, :], TTf[:, e8, :, :], eng="act")
                yield
                for par in range(2):
                    for ch in range(CB):
                        cs_ = slice(ch * 64, (ch + 1) * 64)
                        ps_ = slice(par * 64, (par + 1) * 64)
                        P.mm(bM[ps_, ch * 128:(ch + 1) * 128], Kt[ps_, e8, cs_], AR[ps_, e8, :, cs_])
                P.tt(LM[:, e8, :, :, :].rearrange("p c a t -> p c (a t)"),
                     bM[:, :].rearrange("p (c x) -> p c x", x=128), mk3, ALU.mult)
                yield
                for par in range(2):
                    for ch in range(CB):
                        cs_ = slice(ch * 64, (ch + 1) * 64)
                        ps_ = slice(par * 64, (par + 1) * 64)
                        P.mm(bM[ps_, ch * 64:(ch + 1) * 64], AR[ps_, e8, 0, cs_], Bt[ps_, e8, cs_])
                P.tt(Nm[0][:, e8, :, :], bM[:, 0:CB * 64].rearrange("p (c x) -> p c x", x=64),
                     slm[:, :].unsqueeze(1).to_broadcast([128, CB, 64]), ALU.mult)

            def doubling(ecs):
                for m in range(1, DBG.get('lv', 6)):
                    a, b = (m - 1) % 2, m % 2
                    for i, ec in enumerate(ecs):
                        e8 = ec % G8
                        bkT = P.bank[3 + (i % 2) * 2]
                        bkN = P.bank[4 + (i % 2) * 2]
                        for par in range(2):
                            for ch in range(CB):
                                ps_ = slice(par * 64, (par + 1) * 64)
                                P.mm(bkN[ps_, ch * 64:(ch + 1) * 64], NTm[a][ps_, e8, ch, :], Nm[a][ps_, e8, ch, :])
                                if m < 5:
                                    P.mm(bkT[ps_, ch * 64:(ch + 1) * 64], Nm[a][ps_, e8, ch, :], NTm[a][ps_, e8, ch, :])
                        P.copy(Nm[b][:, e8, :, :], bkN[:, 0:256].rearrange("p (c x) -> p c x", x=64), eng="dve")
                        if m < 5:
                            P.copy(NTm[b][:, e8, :, :], bkT[:, 0:256].rearrange("p (c x) -> p c x", x=64), eng=DBG.get("ntm_eng", "act"))
                    for i, ec in enumerate(ecs if DBG.get('tt', 1) else []):
                        e8 = ec % G8
                        bkN = P.bank[4 + (i % 2) * 2]
                        for par in range(2):
                            for ch in range(CB):
                                ps_ = slice(par * 64, (par + 1) * 64)
                                P.mm(bkN[ps_, 256 + ch * 64:256 + (ch + 1) * 64], Nm[b][ps_, e8, ch, :], TTb[ps_, e8, ch, :])
                        P.tt(TTf[:, e8, :, :], bkN[:, 256:512].rearrange("p (c x) -> p c x", x=64), TTf[:, e8, :, :], ALU.add)
                        P.copy(TTb[:, e8, :, :], TTf[:, e8, :, :], eng="act")
                    yield

            def chain(g, blk):
                ecs = list(range(g * G8, (g + 1) * G8))
                bX, bU, bY, bZ = P.bank[0], P.bank[1], P.bank[6], P.bank[7]
                for ch in range(CB):
                    cs_ = slice(ch * 64, (ch + 1) * 64)
                    for par in range(2):
                        for e8, ec in enumerate(ecs):
                            ps_ = slice(par * 64, (par + 1) * 64)
                            o_ = bX[ps_, e8 * 64:(e8 + 1) * 64]
                            P.mm(o_, AR[ps_, e8, 0, cs_], Zb[ps_, ec, :], start=True, stop=False)
                            P.mm(o_, LM[ps_, e8, ch, 0, :], TM[ps_, e8, ch, 2, :], start=False, stop=True)
                    P.copy(Xb[:, :, :], bX[:, :].rearrange("p (e v) -> p e v", v=64), eng="act")
                    for par in range(2):
                        for e8, ec in enumerate(ecs):
                            ps_ = slice(par * 64, (par + 1) * 64)
                            P.mm(bU[ps_, e8 * 64:(e8 + 1) * 64], TTb[ps_, e8, ch, :], Xb[ps_, e8, :])
                    P.copy(Ub[:, :, :], bU[:, :].rearrange("p (e v) -> p e v", v=64), eng="dve")
                    for par in range(2):
                        for e8, ec in enumerate(ecs):
                            ps_ = slice(par * 64, (par + 1) * 64)
                            o_ = bY[ps_, e8 * 64:(e8 + 1) * 64]
                            P.mm(o_, Zb[ps_, ec, :], AR[ps_, e8, 1, cs_], start=True, stop=False)
                            P.mm(o_, Ub[ps_, e8, :], NTM[ps_, e8, ch, 1, :], start=False, stop=False)
                            P.mm(o_, TM[ps_, e8, ch, 2, :], LM[ps_, e8, ch, 1, :], start=False, stop=True)
                            o2 = bZ[ps_, e8 * 64:(e8 + 1) * 64]
                            P.mm(o2, TM[ps_, e8, ch, 0, :], Ub[ps_, e8, :], start=True, stop=False)
                            P.mm(o2, TM[ps_, e8, ch, 1, :], TM[ps_, e8, ch, 2, :], start=False, stop=True)
                    P.copy(Ysb[:, :, cs_], bY[:, :].rearrange("p (e t) -> p e t", t=64), eng="act")
                    zs = Zf[:, g * G8:(g + 1) * G8, :]
                    P.tt(zs, zs, gC[:, :, ch:ch + 1].to_broadcast([128, G8, 64]), ALU.mult)
                    P.tt(zs, bZ[:, :].rearrange("p (e v) -> p e v", v=64), zs, ALU.add)
                    P.copy(Zb[:, g * G8:(g + 1) * G8, :], zs, eng="act")

            def finalize(ec, blk):
                e8 = ec % G8
                t0 = blk * TB
                W = tmps[ec % NSET]
                col = slice(ec, ec + 1)
                bS = P.bank[2 + ec % 2]
                y = Ysb[:, e8, :]
                P.tt(W["y2"][:, :], y, y, ALU.mult, eng="pool")
                yield
                P.mm(bS[:, 0:TB], bo32[:, :], y)
                P.mm(bS[:, TB:2 * TB], bo32[:, :], W["y2"][:, :])
                P.copy(W["mean"][:, :], bS[:, 0:TB], eng="dve")
                P.tt(W["msq"][:, :], W["mean"][:, :], W["mean"][:, :], ALU.mult, eng="pool")
                P.tt(W["var"][:, :], bS[:, TB:2 * TB], W["msq"][:, :], ALU.subtract)
                yield
                P.act(W["sd"][:, :], W["var"][:, :], AF.Sqrt, bias=gne[:, 0:1])
                P.recip(W["rstd"][:, :], W["sd"][:, :])
                yield
                P.tt(W["dd"][:, :], y, W["mean"][:, :], ALU.subtract, eng="pool")
                P.tt(W["dd"][:, :], W["dd"][:, :], W["rstd"][:, :], ALU.mult)
                yield
                P.ts(W["yn"][:, :], W["dd"][:, :], ln_w[:, col], ALU.mult, ln_b[:, col], ALU.add)
                P.tt(W["yn"][:, :], W["yn"][:, :], bonus[:, e8, :], ALU.add, eng="pool")
                P.tt(W["yo"][:, :], W["yn"][:, :], gate[:, e8, :], ALU.mult, eng="pool")
                P.dma(S["y"][ec * 128:(ec + 1) * 128, t0:t0 + TB], W["yo"][:, :])

            for blk in range(DBG.get("nblk", T // TB)):
                for g in range(DBG.get("ng", 16 // G8)):
                    ecs = list(range(g * G8, (g + 1) * G8))
                    for i in range(0, G8, NSET):
                        interleave([prep(ec, blk) for ec in ecs[i:i + NSET]])
                    interleave([doubling(ecs)])
                    chain(g, blk)
                    for i in range(0, G8, NSET):
                        interleave([finalize(ec, blk) for ec in ecs[i:i + NSET]])
    phase_outproj(P, C, S["y"], w_out_d, x_d, x1_d)


FUSED = 1
_CACHE = {}


def _decl_l0(P, x1_kind):
    d = lambda n, sh, kind="ExternalInput", dt_=F32: P.dram(n, sh, dt_, kind=kind).ap()
    a = dict(x_d=d("x", [T, D]), nw_d=d("nw0", [D]), mu_d=d("mu", [6, D]), w_in_d=d("w_in0", [D, 4 * D]),
             w0_d=d("w0", [D]), w1_d=d("w1", [D, 96]), w2_d=d("w2", [96, D]),
             a0_d=d("a0", [D]), a1_d=d("a1", [D, 96]), a2_d=d("a2", [96, D]),
             kk_d=d("k_k", [D]), ka_d=d("k_a", [D]), rk_d=d("r_k", [D]), lnw_d=d("ln_w", [D]), lnb_d=d("ln_b", [D]),
             w_out_d=d("w_out0", [D, D]))
    a["x1_d"] = d("x1", [T, D], kind=x1_kind)
    S = {k: P.dram("s0_" + k, [D, T], F32).ap() for k in ("r", "k", "v", "g")}
    S["y"] = P.dram("s0_y", [D, T], BF16).ap()
    a["S"] = S
    return a


def _decl_l1(P, x1_ap):
    d = lambda n, sh, kind="ExternalInput", dt_=F32: P.dram(n, sh, dt_, kind=kind).ap()
    a = dict(x1_d=x1_ap if x1_ap is not None else d("x1", [T, D]), nw_d=d("nw1", [D]), w_in_d=d("w_in1", [D, 4 * D]),
             lam_d=d("lam", [4, 64]), subln_d=d("subln", [128]), w_out_d=d("w_out1", [D, D]), fnw_d=d("fnw", [D]),
             out_d=d("out", [T, D], kind="ExternalOutput"))
    S = {k: P.dram("s1_" + k, [D, T], BF16).ap() for k in ("q", "k", "y")}
    S["v"] = P.dram("s1_v", [T, D], BF16).ap()
    S["g"] = P.dram("s1_g", [T, D], BF16).ap()
    a["S"] = S
    a["lambda_init"] = 0.8 - 0.6 * math.exp(-0.3 * 1)
    return a


def _build_fused():
    P, C = setup_prog()
    a0 = _decl_l0(P, "Internal")
    layer0(P, C, **a0)
    a1 = _decl_l1(P, a0["x1_d"])
    layer1(P, C, a1["x1_d"], a1["nw_d"], a1["w_in_d"], a1["lam_d"], a1["subln_d"], a1["w_out_d"], a1["fnw_d"],
           a1["out_d"], a1["lambda_init"], a1["S"])
    P.finish([a1["out_d"]])
    return P.build()


def _build_l0():
    P, C = setup_prog()
    a0 = _decl_l0(P, "ExternalOutput")
    layer0(P, C, **a0)
    P.finish([a0["x1_d"]])
    return P.build()


def _build_l1():
    P, C = setup_prog()
    a1 = _decl_l1(P, None)
    layer1(P, C, a1["x1_d"], a1["nw_d"], a1["w_in_d"], a1["lam_d"], a1["subln_d"], a1["w_out_d"], a1["fnw_d"],
           a1["out_d"], a1["lambda_init"], a1["S"])
    P.finish([a1["out_d"]])
    return P.build()


def _get(name, fn):
    if name not in _CACHE:
        _CACHE[name] = fn()
    return _CACHE[name]


def kernel(x, norm_w, rwkv_mu, rwkv_w_in, rwkv_w0, rwkv_w1, rwkv_w2, rwkv_a0, rwkv_a1, rwkv_a2,
           rwkv_k_k, rwkv_k_a, rwkv_r_k, rwkv_ln_w, rwkv_ln_b, rwkv_w_out,
           diff_w_in, diff_lambda, diff_subln_w, diff_w_out, final_norm_w):
    f = lambda v: np.ascontiguousarray(np.asarray(v), dtype=np.float32)
    x = f(x)
    n = x.shape[0]
    m0 = {"nw0": f(norm_w)[0], "mu": f(rwkv_mu)[0], "w_in0": f(rwkv_w_in)[0],
          "w0": f(rwkv_w0)[0], "w1": f(rwkv_w1)[0], "w2": f(rwkv_w2)[0],
          "a0": f(rwkv_a0)[0], "a1": f(rwkv_a1)[0], "a2": f(rwkv_a2)[0],
          "k_k": f(rwkv_k_k)[0], "k_a": f(rwkv_k_a)[0], "r_k": f(rwkv_r_k)[0].reshape(-1),
          "ln_w": f(rwkv_ln_w)[0], "ln_b": f(rwkv_ln_b)[0], "w_out0": f(rwkv_w_out)[0]}
    m1 = {"nw1": f(norm_w)[1], "w_in1": f(diff_w_in)[0], "lam": f(diff_lambda)[0],
          "subln": f(diff_subln_w)[0], "w_out1": f(diff_w_out)[0], "fnw": f(final_norm_w)}
    cores = list(range(n))
    if FUSED:
        nc = _get("fused", _build_fused)
        maps = [dict(m0, **m1, x=x[i]) for i in range(n)]
        res = run_bass_kernel_spmd(nc, maps, core_ids=cores)
        return np.stack([np.asarray(res.results[i]["out"], dtype=np.float32) for i in range(n)], 0)
    nc0 = _get("l0", _build_l0)
    res0 = run_bass_kernel_spmd(nc0, [dict(m0, x=x[i]) for i in range(n)], core_ids=cores)
    nc1 = _get("l1", _build_l1)
    res1 = run_bass_kernel_spmd(nc1, [dict(m1, x1=np.asarray(res0.results[i]["x1"], dtype=np.float32)) for i in range(n)],
                                core_ids=cores)
    return np.stack([np.asarray(res1.results[i]["out"], dtype=np.float32) for i in range(n)], 0)
```

```python
import contextlib
import numpy as np
import concourse.bass as bass
import concourse.mybir as mybir
from concourse.bass_utils import run_bass_kernel_spmd

dt = mybir.dt
F32, BF16, I32 = dt.float32, dt.bfloat16, dt.int32
AF = mybir.ActivationFunctionType
ALU = mybir.AluOpType
AX = mybir.AxisListType
_DSZ = {F32: 4, BF16: 2, I32: 4, dt.uint32: 4, dt.int16: 2, dt.uint16: 2, dt.uint8: 1, dt.int8: 1,
        dt.float16: 2}


def _isap(x):
    return hasattr(x, "tensor") and hasattr(x, "ap")


class Prog:
    NQ = 8

    def __init__(self):
        nc = self.nc = bass.Bass("TRN2", target_bir_lowering=False)
        self.E = dict(pe=nc.tensor, dve=nc.vector, act=nc.scalar, pool=nc.gpsimd, sp=nc.sync)
        self.semh = {}
        for e in self.E:
            self.semh[e] = nc.alloc_semaphore("c_" + e)
        self.cnt = {e: 0 for e in self.E}
        self.stream = {e: [] for e in self.E}
        self.seen = {e: {} for e in self.E}
        self.dq = {}
        for q in ("sp", "act", "pool"):
            for i in range(self.NQ):
                self.semh[f"d_{q}_{i}"] = nc.alloc_semaphore(f"d_{q}_{i}")
            self.dq[q] = 0
        self.acc = {}
        self.nops = 0
        self._uid = 0
        self._stacks = []
        self._pemode = None
        self._perow = {}

    def sb(self, name, shape, dtype=F32):
        self._uid += 1
        name = f"{name}_{self._uid}"
        if self._stacks:
            return self._stacks[-1].enter_context(self.nc.sbuf_tensor(name, list(shape), dtype))
        return self.nc.alloc_sbuf_tensor(name, list(shape), dtype)

    @contextlib.contextmanager
    def scope(self):
        st = contextlib.ExitStack()
        self._stacks.append(st)
        try:
            yield
        finally:
            self.barrier()
            self._stacks.pop()
            st.close()

    def barrier(self):
        targets = {e: self.cnt[e] for e in self.E}
        for q, n in self.dq.items():
            for i in range(self.NQ):
                if n > i:
                    targets[f"d_{q}_{i}"] = 16 * ((n - 1 - i) // self.NQ + 1)
        for e in self.E:
            seen = self.seen[e]
            waits = []
            for k, v in targets.items():
                if v > seen.get(k, 0) and not (k == e):
                    seen[k] = v
                    waits.append((self.semh[k], v))
            self.cnt[e] += 1
            en = self.E[e]
            for sh, v in waits:
                en.wait_ge(sh, v)
            en.nop().then_inc(self.semh[e], 1)
        self.acc.clear()

    def ps(self, name, shape=(128, 512), dtype=F32):
        return self.nc.alloc_psum_tensor(name, list(shape), dtype)

    def dram(self, name, shape, dtype=F32, kind="Internal"):
        return self.nc.dram_tensor(name, list(shape), dtype, kind=kind)

    @staticmethod
    def _region(a):
        sz = _DSZ[a.dtype]
        ap = a.ap
        off = a.offset
        if str(a.space) == "DRAM":
            ext = 1
            for st, c in ap:
                ext += (c - 1) * abs(st)
            return a.tensor.name, 0, 1, off * sz, (off + ext) * sz
        pstep, pc = ap[0]
        if str(a.space) == "PSUM":
            return a.tensor.name, 0, 128, 0, 1 << 30
        if pstep == 0:
            p0, f0 = 0, off
            pstep = 1 << 60
        else:
            p0, f0 = off // pstep, off % pstep
        ext = 1
        for st, c in ap[1:]:
            ext += (c - 1) * abs(st)
        return a.tensor.name, p0, p0 + pc, f0 * sz, (f0 + ext) * sz

    def _collect(self, reg, is_write, deps):
        name, p0, p1, f0, f1 = reg
        rec = self.acc.get(name)
        if rec is None:
            return
        lists = (rec[0], rec[1]) if is_write else (rec[0],)
        for lst in lists:
            for (q0, q1, g0, g1, k, v) in lst:
                if q0 < p1 and p0 < q1 and g0 < f1 and f0 < g1:
                    if deps.get(k, 0) < v:
                        deps[k] = v

    def _record(self, reg, is_write, dep):
        name, p0, p1, f0, f1 = reg
        rec = self.acc.get(name)
        if rec is None:
            rec = self.acc[name] = ([], [])
        if is_write:
            for i in (0, 1):
                lst = rec[i]
                if lst:
                    lst[:] = [r for r in lst if not (p0 <= r[0] and r[1] <= p1 and f0 <= r[2] and r[3] <= f1)]
            rec[0].append((p0, p1, f0, f1, dep[0], dep[1]))
        else:
            lst = rec[1]
            for i, r in enumerate(lst):
                if r[4] == dep[0] and r[0] == p0 and r[1] == p1 and r[2] == f0 and r[3] == f1:
                    lst[i] = (p0, p1, f0, f1, dep[0], max(dep[1], r[5]))
                    return
            lst.append((p0, p1, f0, f1, dep[0], dep[1]))

    def op(self, eng, fn, reads=(), writes=(), dma=False):
        self.nops += 1
        rregs = [self._region(a) for a in reads if _isap(a)]
        wregs = [self._region(a) for a in writes if _isap(a)]
        pr = [r for r in rregs if r[4] == 1 << 30]
        if pr:
            rregs = [r for r in rregs if r[4] != 1 << 30]
            wregs = wregs + pr
        deps = {}
        for r in rregs:
            self._collect(r, False, deps)
        for r in wregs:
            self._collect(r, True, deps)
        if dma:
            n = self.dq[eng]
            self.dq[eng] += 1
            key = f"d_{eng}_{n % self.NQ}"
            val = 16 * (n // self.NQ + 1)
            if n >= self.NQ and deps.get(key, 0) < val - 16:
                deps[key] = val - 16
            mydep = (key, val)
            inc = 16
        else:
            self.cnt[eng] += 1
            mydep = (eng, self.cnt[eng])
            inc = 1
        waits = []
        seen = self.seen[eng]
        for k, v in deps.items():
            if k == "pe" and eng == "pe":
                continue
            if seen.get(k, 0) >= v:
                continue
            seen[k] = v
            waits.append((self.semh[k], v))
        en = self.E[eng]
        for sh, v in waits:
            en.wait_ge(sh, v)
        fn(en).then_inc(self.semh[mydep[0]], inc)
        for r in rregs:
            self._record(r, False, mydep)
        for r in wregs:
            self._record(r, True, mydep)

    def build(self):
        nc = self.nc

        return nc

    def dma(self, out, in_, q="sp", **kw):
        self.op(q, lambda e: e.dma_start(out=out, in_=in_, **kw), reads=[in_], writes=[out], dma=True)

    def _pe_mode(self, lhsT):
        def rnd(n):
            return 32 if n <= 32 else (64 if n <= 64 else 128)
        k = lhsT.ap[0][1]
        m = 1
        for st, c in lhsT.ap[1:]:
            m *= c
        mode = (rnd(k), rnd(m))
        if mode != self._pemode:
            if self._pemode is not None:
                self.E["pe"].drain()
            self._pemode = mode

    def _pe_rowgrp(self, out, lhsT):
        k = lhsT.ap[0][1]
        rg = (lhsT.offset // lhsT.ap[0][0]) if k <= 64 else -1
        name = out.tensor.name
        last = self._perow.get(name)
        if last is not None and last[0] != rg and self.seen["pe"].get("pe", 0) < last[1]:
            self.seen["pe"]["pe"] = last[1]
            self.E["pe"].wait_ge(self.semh["pe"], last[1])
        self._perow[name] = (rg, self.cnt["pe"] + 1)

    def mm(self, out, lhsT, rhs, start=True, stop=True, **kw):
        self._pe_mode(lhsT)
        self._pe_rowgrp(out, lhsT)
        self.op("pe", lambda e: e.matmul(out, lhsT, rhs, start=start, stop=stop, **kw),
                reads=[lhsT, rhs], writes=[out])

    def tr(self, out, in_, ident):
        self._pe_mode(in_)
        self._pe_rowgrp(out, in_)
        self.op("pe", lambda e: e.transpose(out, in_, ident), reads=[in_, ident], writes=[out])

    def act(self, out, in_, func, bias=0.0, scale=1.0, accum_out=None, eng="act"):
        kw = {}
        if accum_out is not None:
            kw["accum_out"] = accum_out
        self.op("act", lambda e: e.activation(out, in_, func, bias=bias, scale=scale, **kw),
                reads=[in_, bias, scale], writes=[out, accum_out])

    def tt(self, out, in0, in1, op, eng="dve"):
        self.op(eng, lambda e: e.tensor_tensor(out, in0, in1, op), reads=[in0, in1], writes=[out])

    def ts(self, out, in0, s1, op0, s2=None, op1=None, eng="dve", accum_out=None):
        kw = {}
        if op1 is not None:
            kw["op1"] = op1
        if accum_out is not None:
            kw["accum_out"] = accum_out
        self.op(eng, lambda e: e.tensor_scalar(out, in0, s1, s2, op0, **kw),
                reads=[in0, s1, s2], writes=[out, accum_out])

    def stt(self, out, in0, scalar, in1, op0, op1, eng="dve"):
        self.op(eng, lambda e: e.scalar_tensor_tensor(out, in0, scalar, in1, op0, op1),
                reads=[in0, scalar, in1], writes=[out])

    def copy(self, out, in_, eng="dve"):
        if eng == "act":
            self.op("act", lambda e: e.copy(out, in_), reads=[in_], writes=[out])
        else:
            self.op(eng, lambda e: e.tensor_copy(out, in_), reads=[in_], writes=[out])

    def memset(self, ap, val, eng="dve"):
        self.op(eng, lambda e: e.memset(ap, val), writes=[ap])

    def recip(self, out, in_):
        self.op("dve", lambda e: e.reciprocal(out, in_), reads=[in_], writes=[out])

    def reduce(self, out, in_, op=ALU.add, axis=AX.X):
        self.op("dve", lambda e: e.tensor_reduce(out, in_, axis, op), reads=[in_], writes=[out])

    def scan(self, out, d0, d1, initial, op0, op1):
        self.op("dve", lambda e: e.tensor_tensor_scan(out, d0, d1, initial, op0, op1),
                reads=[d0, d1, initial], writes=[out])

    def finish(self, out_aps):
        self.op("sp", lambda e: e.nop(), reads=list(out_aps))


import math

T = 2048
D = 2048
NCH = 16
EPS = 1e-6


def make_consts(P):
    C = {}
    ones = P.sb("c_ones", [128, 128], BF16)
    ident = P.sb("c_ident", [128, 128], BF16)
    P.memset(ones[:, :], 1.0, eng="pool")
    P.op("pool", lambda e: e.affine_select(ident[:, :], ones[:, :], [[-1, 128]], ALU.is_equal, 0.0,
                                           base=0, channel_multiplier=1),
         reads=[ones[:, :]], writes=[ident[:, :]])
    C["ones"] = ones
    C["ident"] = ident
    return C


def load_w_cast(P, wbf, w_d, col0, ncols):
    wv = w_d.rearrange("(c p) e -> p c e", p=128)
    for c4 in range(4):
        P.dma(wbf[:, c4 * 4:(c4 + 1) * 4, 0:ncols], wv[:, c4 * 4:(c4 + 1) * 4, col0:col0 + ncols], q="pool")


def phase_norm(P, C, x_d, nw_d, hT, col_off=0):
    with P.scope():
        nw = P.sb("nw", [128, 16], F32)
        P.dma(nw[:, :], nw_d.rearrange("(c p) -> p c", p=128), allow_slow_non_contiguous=True)
        xt = [P.sb(f"xt{i}", [128, D], F32) for i in range(2)]
        xn = [P.sb(f"xn{i}", [128, D], BF16) for i in range(2)]
        junk = P.sb("junk", [128, D], BF16)
        ss = [P.sb(f"ss{i}", [128, 4], F32) for i in range(2)]
        for tt in range(T // 128):
            b = tt % 2
            P.dma(xt[b][:, :], x_d[tt * 128:(tt + 1) * 128, :])
            P.act(junk[:, :], xt[b][:, :], AF.Square, accum_out=ss[b][:, 0:1])
            P.act(ss[b][:, 1:2], ss[b][:, 0:1], AF.Sqrt, bias=C["epsn"][:, 0:1], scale=1.0 / D)
            P.recip(ss[b][:, 2:3], ss[b][:, 1:2])
            P.ts(xn[b][:, :], xt[b][:, :], ss[b][:, 2:3], ALU.mult)
            for half in range(2):
                ps = P.bank[half].bitcast(BF16)
                for j in range(8):
                    c = half * 8 + j
                    P.tr(ps[:, j * 128:(j + 1) * 128], xn[b][:, c * 128:(c + 1) * 128], C["ident"][:, :])
                P.tt(hT[:, half * 8:(half + 1) * 8, col_off + tt * 128:col_off + (tt + 1) * 128],
                     ps[:, :].rearrange("p (c t) -> p c t", c=8),
                     nw[:, half * 8:(half + 1) * 8].unsqueeze(2).to_broadcast([128, 8, 128]), ALU.mult)


def proj_fm(P, wbuf, w_d, col0, rhs_fn, evac_fn, n_out_chunks=16, tbs=512):
    bi = 0
    for eg in range(n_out_chunks // 4):
        wbf = wbuf[eg % 2]
        load_w_cast(P, wbf, w_d, col0 + eg * 512, 512)
        for e4 in range(4):
            ec = eg * 4 + e4
            for tb in range(T // tbs):
                ps = P.bank[2 + bi % 4]
                bi += 1
                for c in range(NCH):
                    P.mm(ps[:, 0:tbs], wbf[:, c, e4 * 128:(e4 + 1) * 128], rhs_fn(c, tb * tbs, tbs),
                         start=(c == 0), stop=(c == NCH - 1))
                evac_fn(ec, tb, ps[:, 0:tbs])


def proj_tm(P, wbuf, w_d, col0, hT, hoff, evac_fn):
    bi = 0
    for eg in range(4):
        wbf = wbuf[eg % 2]
        load_w_cast(P, wbf, w_d, col0 + eg * 512, 512)
        for tt in range(T // 128):
            ps = P.bank[2 + bi % 4]
            bi += 1
            for c in range(NCH):
                P.mm(ps[:, :], hT[:, c, hoff + tt * 128:hoff + (tt + 1) * 128], wbf[:, c, :],
                     start=(c == 0), stop=(c == NCH - 1))
            evac_fn(tt, eg, ps[:, :])


def phase_outproj(P, C, yT_src, w_out_d, xres_d, out_d, fnw_d=None):
    with P.scope():
        wo = P.sb("wo", [128, 16, D], BF16)
        wv = w_out_d.rearrange("(c p) e -> p c e", p=128)
        for c in range(16):
            P.dma(wo[:, c, :], wv[:, c, :], q="pool")
        yT = P.sb("yTo", [128, 16, T], BF16)
        yv = yT_src.rearrange("(c p) t -> p c t", p=128)
        for c in range(16):
            P.dma(yT[:, c, :], yv[:, c, :])
        if fnw_d is not None:
            fnw = P.sb("fnw", [128, D], F32)
            P.dma(fnw[:, :], fnw_d.rearrange("(o d) -> o d", o=1).partition_broadcast(128).rearrange("p o d -> p (o d)"))
        xr = [P.sb(f"xr{i}", [128, D], F32) for i in range(2)]
        ot = [P.sb(f"ot{i}", [128, D], F32) for i in range(2)]
        junk = P.sb("junk2", [128, D], BF16)
        ss = [P.sb(f"ss2{i}", [128, 4], F32) for i in range(2)]
        bi = 0
        for tt in range(T // 128):
            b = tt % 2
            P.dma(xr[b][:, :], xres_d[tt * 128:(tt + 1) * 128, :])
            for eg in range(4):
                ps = P.bank[bi % 4]
                bi += 1
                for c in range(NCH):
                    P.mm(ps[:, :], yT[:, c, tt * 128:(tt + 1) * 128], wo[:, c, eg * 512:(eg + 1) * 512],
                         start=(c == 0), stop=(c == NCH - 1))
                P.tt(xr[b][:, eg * 512:(eg + 1) * 512], ps[:, :], xr[b][:, eg * 512:(eg + 1) * 512], ALU.add)
            if fnw_d is None:
                P.dma(out_d[tt * 128:(tt + 1) * 128, :], xr[b][:, :])
            else:
                P.act(junk[:, :], xr[b][:, :], AF.Square, accum_out=ss[b][:, 0:1])
                P.act(ss[b][:, 1:2], ss[b][:, 0:1], AF.Sqrt, bias=C["epsn"][:, 0:1], scale=1.0 / D)
                P.recip(ss[b][:, 2:3], ss[b][:, 1:2])
                P.stt(ot[b][:, :], xr[b][:, :], ss[b][:, 2:3], fnw[:, :], ALU.mult, ALU.mult, eng="pool" if False else "dve")
                P.dma(out_d[tt * 128:(tt + 1) * 128, :], ot[b][:, :])


def layer1(P, C, x1_d, nw_d, w_in_d, lam_d, subln_d, w_out_d, fnw_d, out_d, lambda_init, S):
    H = 16
    scale = 1.0 / 8.0
    with P.scope():
        hT = P.sb("hT1", [128, 16, T], BF16)
        phase_norm(P, C, x1_d, nw_d, hT)
        wbuf = [P.sb(f"wbuf{i}", [128, 16, 512], BF16) for i in range(2)]
        ost = [P.sb(f"ost{i}", [128, 512], BF16) for i in range(4)]
        cnt = [0]

        def mk_evac_fm(dst):
            def f(ec, tb, ps):
                o = ost[cnt[0] % 4]
                if cnt[0] % 2 == 0:
                    P.copy(o[:, :], ps, eng="act")
                else:
                    P.copy(o[:, :], ps, eng="dve")
                cnt[0] += 1
                P.dma(dst[ec * 128:(ec + 1) * 128, tb * 512:(tb + 1) * 512], o[:, :])
            return f

        def mk_evac_tm(dst, silu):
            def f(tt, eg, ps):
                o = ost[cnt[0] % 4]
                if silu:
                    P.act(o[:, :], ps, AF.Silu)
                elif cnt[0] % 2 == 0:
                    P.copy(o[:, :], ps, eng="act")
                else:
                    P.copy(o[:, :], ps, eng="dve")
                cnt[0] += 1
                P.dma(dst[tt * 128:(tt + 1) * 128, eg * 512:(eg + 1) * 512], o[:, :])
            return f

        rhs_fn = lambda c, t0, n: hT[:, c, t0:t0 + n]
        proj_fm(P, wbuf, w_in_d, 0 * D, rhs_fn, mk_evac_fm(S["q"]))
        proj_fm(P, wbuf, w_in_d, 1 * D, rhs_fn, mk_evac_fm(S["k"]))
        proj_tm(P, wbuf, w_in_d, 2 * D, hT, 0, mk_evac_tm(S["v"], False))
        proj_tm(P, wbuf, w_in_d, 3 * D, hT, 0, mk_evac_tm(S["g"], True))

    with P.scope():
        lam = P.sb("lam", [128, 4, 64], F32)
        P.dma(lam[:, :, :].rearrange("p a b -> p (a b)"),
              lam_d.rearrange("(o a) b -> o (a b)", o=1).partition_broadcast(128).rearrange("p o f -> p (o f)"))
        lt = P.sb("lt", [128, 8], F32)
        lj = P.sb("lj", [128, 64], F32)
        for i in range(2):
            P.tt(lj[:, :], lam[:, 2 * i, :], lam[:, 2 * i + 1, :], ALU.mult)
            P.reduce(lt[:, i:i + 1], lj[:, :])
            P.act(lt[:, 2 + i:3 + i], lt[:, i:i + 1], AF.Exp)
        P.tt(lt[:, 4:5], lt[:, 3:4], lt[:, 2:3], ALU.subtract)
        P.ts(lt[:, 5:6], lt[:, 4:5], -lambda_init, ALU.add)
        nlam = lt[:, 5:6]
        slw = P.sb("slw", [128, 128], F32)
        P.dma(slw[:, :], subln_d.rearrange("(o d) -> o d", o=1).partition_broadcast(128).rearrange("p o d -> p (o d)"))
        P.ts(slw[:, :], slw[:, :], 1.0 - lambda_init, ALU.mult)
        kpos_i = P.sb("kpos_i", [128, 16], I32)
        kpos = P.sb("kpos", [128, 16], F32)
        P.op("pool", lambda e: e.iota(kpos_i[:, :], [[128, 16]], base=0, channel_multiplier=1), writes=[kpos_i[:, :]])
        P.copy(kpos[:, :], kpos_i[:, :], eng="dve")
        qpos_i = P.sb("qpos_i", [65, T], I32)
        qpos = P.sb("qpos", [65, T], F32)
        P.op("pool", lambda e: e.iota(qpos_i[64:65, :], [[1, T]], base=0, channel_multiplier=0), writes=[qpos_i[64:65, :]])
        P.copy(qpos[64:65, :], qpos_i[64:65, :], eng="dve")
        bkh = [P.sb(f"bkh{i}", [128, 16], F32) for i in range(2)]
        st4 = P.sb("st4", [128, 32], F32)
        cbig = P.sb("cbig", [128, 128], BF16)
        cmask = P.sb("cmask", [128, 128], BF16)
        P.memset(cbig[:, :], 3.0e38, eng="pool")
        P.op("pool", lambda e: e.affine_select(cmask[:, :], cbig[:, :], [[1, 128]], ALU.is_ge, 0.0,
                                               base=0, channel_multiplier=-1),
             reads=[cbig[:, :]], writes=[cmask[:, :]])
        qa = [[P.sb(f"qa{b}{c}", [65, T], BF16) for c in range(2)] for b in range(2)]
        ka = [[P.sb(f"ka{b}{c}", [65, T], BF16) for c in range(2)] for b in range(2)]
        for b in range(2):
            for c in range(2):
                P.memset(ka[b][c][64:65, :], 1.0, eng="dve")
        va = [P.sb(f"va{b}", [128, 16, 130], BF16) for b in range(2)]
        for b in range(2):
            P.memset(va[b][:, :, 128:130], 1.0, eng="dve")
        gt = [P.sb(f"gt{b}", [128, 16, 128], BF16) for b in range(2)]
        pt = [P.sb(f"pt{i}", [128, 512], BF16) for i in range(3)]
        yT = P.sb("yT", [128, T], BF16)
        yTb = [P.sb(f"yTb{b}", [128, T], BF16) for b in range(2)]
        o0 = P.sb("o0", [128, 128], F32)
        o1 = P.sb("o1", [128, 128], F32)
        at = P.sb("at", [128, 128], F32)
        yb = P.sb("yb", [128, 128], BF16)
        junk = P.sb("junk3", [128, 128], F32)
        sbank = [P.bank[0], P.bank[1], P.bank[2]]
        accb = [[P.bank[3], P.bank[4]], [P.bank[5], P.bank[6]]]
        trb = P.bank[7].bitcast(BF16)
        at4 = [P.sb(f"at4{i}", [128, 128], F32) for i in range(4)]

        def head_setup(h):
            b = h % 2
            slope = 2.0 ** (-8.0 * (h + 1) / H)
            for c in range(2):
                P.dma(qa[b][c][0:64, :], S["q"][h * 128 + c * 64:h * 128 + (c + 1) * 64, :])
                P.dma(ka[b][c][0:64, :], S["k"][h * 128 + c * 64:h * 128 + (c + 1) * 64, :])
                P.ts(qa[b][c][64:65, :], qpos[64:65, :], -slope / scale, ALU.mult)
            vsrc = S["v"][:, h * 128:(h + 1) * 128].rearrange("(kb p) e -> p kb e", p=128)
            gsrc = S["g"][:, h * 128:(h + 1) * 128].rearrange("(kb p) e -> p kb e", p=128)
            for k4 in range(4):
                P.dma(va[b][:, k4 * 4:(k4 + 1) * 4, 0:128], vsrc[:, k4 * 4:(k4 + 1) * 4, :])
                P.dma(gt[b][:, k4 * 4:(k4 + 1) * 4, :], gsrc[:, k4 * 4:(k4 + 1) * 4, :])
            P.ts(bkh[b][:, :], kpos[:, :], slope, ALU.mult)

        def stage1(it, slot):
            h, qg, c, kb = it
            b = h % 2
            if qg == 0 and c == 0 and kb == 0:
                head_setup(h)
            q0 = max(kb, 4 * qg) * 128
            q1 = (4 * qg + 4) * 128
            n = q1 - q0
            sb_ = sbank[slot]
            p_ = pt[slot]
            P.mm(sb_[:, 0:n], ka[b][c][0:65, kb * 128:(kb + 1) * 128], qa[b][c][0:65, q0:q1])
            P.act(p_[:, 0:n], sb_[:, 0:n], AF.Exp, bias=bkh[b][:, kb:kb + 1], scale=scale)
            if kb >= 4 * qg:
                P.tt(p_[:, 0:128], p_[:, 0:128], cmask[:, :], ALU.min)

        def stage2(it, slot):
            h, qg, c, kb = it
            b = h % 2
            q0 = max(kb, 4 * qg) * 128
            p_ = pt[slot]
            for qb in range(max(kb, 4 * qg), 4 * qg + 4):
                qq = qb - 4 * qg
                col = qb * 128 - q0
                acc = accb[c][qq // 2][:, (qq % 2) * 256:(qq % 2) * 256 + 130]
                P.mm(acc, p_[:, col:col + 128], va[b][:, kb, :], start=(kb == 0 and qq % 2 == 0), stop=(kb == qb))
            if c == 1 and kb == 4 * qg + 3:
                finalize_group(h, qg)

        def finalize_group(h, qg):
            b = h % 2
            for qq in range(4):
                a0 = accb[0][qq // 2][:, (qq % 2) * 256:(qq % 2) * 256 + 130]
                a1 = accb[1][qq // 2][:, (qq % 2) * 256:(qq % 2) * 256 + 130]
                s_ = st4[:, qq * 8:(qq + 1) * 8]
                P.recip(s_[:, 0:1], a0[:, 128:129])
                P.recip(s_[:, 1:2], a1[:, 128:129])
                P.tt(s_[:, 2:3], s_[:, 1:2], nlam, ALU.mult)
                P.ts(o0[:, :], a0[:, 0:128], s_[:, 0:1], ALU.mult)
                P.stt(at4[qq][:, :], a1[:, 0:128], s_[:, 2:3], o0[:, :], ALU.mult, ALU.add)
            for qq in range(4):
                qb = 4 * qg + qq
                s_ = st4[:, qq * 8:(qq + 1) * 8]
                at = at4[qq]
                P.tt(junk[:, :], at[:, :], at[:, :], ALU.mult, eng="pool")
                P.reduce(s_[:, 3:4], junk[:, :])
                P.act(s_[:, 4:5], s_[:, 3:4], AF.Ln, bias=C["epss"][:, 0:1], scale=1.0 / 128)
                P.act(s_[:, 5:6], s_[:, 4:5], AF.Exp, scale=-0.5)
                P.stt(o1[:, :], at[:, :], s_[:, 5:6], slw[:, :], ALU.mult, ALU.mult)
                P.tt(yb[:, :], o1[:, :], gt[b][:, qb, :], ALU.mult)
                P.tr(trb[:, qq * 128:(qq + 1) * 128], yb[:, :], C["ident"][:, :])
            P.copy(yTb[b][:, qg * 512:(qg + 1) * 512], trb[:, 0:512], eng="dve")
            if qg == 3:
                P.dma(S["y"][h * 128:(h + 1) * 128, :], yTb[b][:, :])

        items = [(h, qg, c, kb) for h in range(H) for qg in range(4) for c in range(2) for kb in range(4 * qg + 4)]
        pendq = []
        for idx, it in enumerate(items):
            stage1(it, idx % 3)
            pendq.append((it, idx % 3))
            if len(pendq) > 2:
                stage2(*pendq.pop(0))
        while pendq:
            stage2(*pendq.pop(0))
    phase_outproj(P, C, S["y"], w_out_d, x1_d, out_d, fnw_d=fnw_d)


def setup_prog():
    P = Prog()
    P.bank = [P.ps(f"bank{i}") for i in range(8)]
    C = make_consts(P)
    epsn = P.sb("epsn", [128, 1], F32)
    P.memset(epsn[:, :], EPS)
    epss = P.sb("epss", [128, 1], F32)
    P.memset(epss[:, :], 1e-5)
    C["epsn"] = epsn
    C["epss"] = epss
    return P, C


C0 = math.exp(-0.5)
GN_EPS = 64e-5
TB = 256
CB = 4
G8 = 8
DBG = {}


def layer0(P, C, x_d, nw_d, mu_d, w_in_d, w0_d, w1_d, w2_d, a0_d, a1_d, a2_d, kk_d, ka_d, rk_d,
           lnw_d, lnb_d, w_out_d, x1_d, S):
    with P.scope():
        xw1T = P.sb("xw1T", [96, T], BF16)
        xa1T = P.sb("xa1T", [96, T], BF16)
        with P.scope():
            hT = P.sb("hT0", [128, 16, T + 1], BF16)
            P.memset(hT[:, :, 0:1], 0.0)
            phase_norm(P, C, x_d, nw_d, hT, col_off=1)
            mu = P.sb("mu", [128, 6, 16], F32)
            om = P.sb("om", [128, 6, 16], F32)
            for s in range(6):
                P.dma(mu[:, s, :], mu_d[s, :].rearrange("(c p) -> p c", p=128), allow_slow_non_contiguous=True)
            P.ts(om[:, :, :], mu[:, :, :], -1.0, ALU.mult, 1.0, ALU.add)
            xp = P.sb("xp", [128, 16, T], BF16)
            tmp = [P.sb(f"xtmp{i}", [128, T], BF16) for i in range(2)]
            wbuf = [P.sb(f"wbuf0{i}", [128, 16, 512], BF16) for i in range(2)]
            ost = [P.sb(f"ost0{i}", [128, 512], F32) for i in range(4)]
            w1b = P.sb("w1b", [128, 16, 96], BF16)
            a1b = P.sb("a1b", [128, 16, 96], BF16)
            P.dma(w1b[:, :, :], w1_d.rearrange("(c p) r -> p c r", p=128), q="pool")
            P.dma(a1b[:, :, :], a1_d.rearrange("(c p) r -> p c r", p=128), q="pool")
            cnt = [0]

            def make_xp(path):
                for c in range(16):
                    tb_ = tmp[c % 2]
                    P.act(tb_[:, :], hT[:, c, 1:T + 1], AF.Copy, scale=om[:, path, c:c + 1])
                    P.stt(xp[:, c, :], hT[:, c, 0:T], mu[:, path, c:c + 1], tb_[:, :], ALU.mult, ALU.add)

            def mk_evac(dst):
                def f(ec, tb, ps):
                    o = ost[cnt[0] % 4]
                    if cnt[0] % 2 == 0:
                        P.copy(o[:, :], ps, eng="act")
                    else:
                        P.copy(o[:, :], ps, eng="dve")
                    cnt[0] += 1
                    P.dma(dst[ec * 128:(ec + 1) * 128, tb * 512:(tb + 1) * 512], o[:, :])
                return f

            rhs_fn = lambda c, t0, n: xp[:, c, t0:t0 + n]
            for path, pidx, key in ((0, 0, "r"), (2, 1, "k"), (3, 2, "v"), (5, 3, "g"))[:DBG.get("nproj", 4)]:
                make_xp(path)
                proj_fm(P, wbuf, w_in_d, pidx * D, rhs_fn, mk_evac(S[key]))
            for path, wl, dst, fn in ((1, w1b, xw1T, AF.Tanh), (4, a1b, xa1T, AF.Copy)):
                make_xp(path)
                for tb in range(4):
                    ps = P.bank[2 + tb]
                    for c in range(NCH):
                        P.mm(ps[0:96, :], wl[:, c, :], xp[:, c, tb * 512:(tb + 1) * 512],
                             start=(c == 0), stop=(c == NCH - 1))
                    P.act(dst[:, tb * 512:(tb + 1) * 512], ps[0:96, :], fn)

        with P.scope():
            def pvec(name, src):
                t_ = P.sb(name, [128, 16], F32)
                P.dma(t_[:, :], src.rearrange("(c p) -> p c", p=128), allow_slow_non_contiguous=True)
                return t_
            w0 = pvec("w0", w0_d)
            a0 = pvec("a0", a0_d)
            k_k = pvec("k_k", kk_d)
            k_a = pvec("k_a", ka_d)
            r_k = pvec("r_k", rk_d)
            ln_w = pvec("ln_w", lnw_d)
            ln_b = pvec("ln_b", lnb_d)
            w2b = P.sb("w2b", [96, D], BF16)
            a2b = P.sb("a2b", [96, D], BF16)
            P.dma(w2b[:, :], w2_d, q="pool")
            P.dma(a2b[:, :], a2_d, q="pool")
            ones32 = P.sb("ones32", [128, 128], F32)
            P.memset(ones32[:, :], 1.0)
            suui = P.sb("suui", [128, 2, 64], F32)
            slm = P.sb("slm", [128, 64], F32)
            i64 = P.sb("i64", [128, 64], F32)
            for par in range(2):
                ps_ = slice(par * 64, (par + 1) * 64)
                for (dst, pat, cm_, cmp_) in ((suui[ps_, 0, :], [[1, 64]], -1, ALU.is_gt),
                                              (suui[ps_, 1, :], [[1, 64]], -1, ALU.is_ge),
                                              (slm[ps_, :], [[-1, 64]], 1, ALU.is_gt),
                                              (i64[ps_, :], [[-1, 64]], 1, ALU.is_equal)):
                    P.op("pool", lambda e, dst=dst, pat=pat, cm_=cm_, cmp_=cmp_, ps_=ps_:
                         e.affine_select(dst, ones32[ps_, 0:64], pat, cmp_, 0.0, base=0, channel_multiplier=cm_),
                         reads=[ones32[ps_, 0:64]], writes=[dst])
            bones = P.sb("bones", [128, 128], BF16)
            bo32 = P.sb("bo32", [128, 128], F32)
            P.memset(bones[:, :], 0.0)
            P.memset(bo32[:, :], 0.0)
            for par in range(2):
                ps_ = slice(par * 64, (par + 1) * 64)
                P.memset(bones[ps_, ps_], 1.0)
                P.memset(bo32[ps_, ps_], 1.0 / 64)
            cm = P.sb("cm", [128, TB], F32)
            P.memset(cm[:, :], 1.0)
            P.memset(cm[:, :].rearrange("p (c t) -> p c t", t=64)[:, :, 0:1], 0.0)
            gne = P.sb("gne", [128, 1], F32)
            P.memset(gne[:, :], GN_EPS)
            ident = C["ident"]
            Zf = P.sb("Zf", [128, 16, 64], F32)
            Zb = P.sb("Zb", [128, 16, 64], BF16)
            P.memset(Zf[:, :, :], 0.0)
            P.memset(Zb[:, :, :], 0.0)
            AR = P.sb("AR", [128, G8, 2, TB], BF16)
            Bt = P.sb("Bt", [128, G8, TB], BF16)
            Kt = P.sb("Kt", [128, G8, TB], BF16)
            TM = P.sb("TM", [128, G8, CB, 3, 64], BF16)
            NTM = P.sb("NTM", [128, G8, CB, 2, 64], BF16)
            LM = P.sb("LM", [128, G8, CB, 2, 64], BF16)
            TTf = P.sb("TTf", [128, G8, CB, 64], F32)
            TTb = P.sb("TTb", [128, G8, CB, 64], BF16)
            Nm = [P.sb(f"Nm{i}", [128, G8, CB, 64], BF16) for i in range(2)]
            NTm = [P.sb(f"NTm{i}", [128, G8, CB, 64], BF16) for i in range(2)]
            bonus = P.sb("bonus", [128, G8, TB], F32)
            gate = P.sb("gate", [128, G8, TB], BF16)
            Ysb = P.sb("Ysb", [128, G8, TB], F32)
            gC = P.sb("gC", [128, G8, CB], F32)
            Xb = P.sb("Xb", [128, G8, 64], BF16)
            Ub = P.sb("Ub", [128, G8, 64], BF16)
            names32 = ["r32", "k32", "v32", "g32", "sg", "ic", "css", "gam", "gin", "d1", "d2", "kk", "lnq", "bb", "u"]
            names16 = ["sq", "bg", "kg", "vb", "rk", "yo"]
            alias = {"gpr": "d1", "gen": "d2", "rn": "lnq", "kkn": "kk", "km": "u",
                     "msq": "sg", "var": "ic", "sd": "css", "rstd": "gam", "dd": "gin", "yn": "bb", "y2": "d2", "mean": "d1"}
            NSET = 4
            tmps = []
            for i in range(NSET):
                d_ = {n: P.sb(f"{n}{i}", [128, TB], F32) for n in names32}
                d_.update({n: P.sb(f"{n}{i}", [128, TB], BF16) for n in names16})
                for k_, v_ in alias.items():
                    d_[k_] = d_[v_]
                tmps.append(d_)

            def interleave(gens):
                gens = list(gens)
                while gens:
                    for g_ in list(gens):
                        try:
                            next(g_)
                        except StopIteration:
                            gens.remove(g_)

            def c3(ap):
                return ap.rearrange("p (c t) -> p c t", t=64)

            def prep(ec, blk):
                e8 = ec % G8
                t0 = blk * TB
                W = tmps[ec % NSET]
                col = slice(ec, ec + 1)
                rows = slice(ec * 128, (ec + 1) * 128)
                pb = ec % 2
                bA = P.bank[0 + pb]
                bB = P.bank[2 + pb]
                bG = P.bank[2 + pb]
                bT = P.bank[4 + pb].bitcast(BF16)
                bM = P.bank[6 + pb]
                for n, key in (("r32", "r"), ("k32", "k"), ("v32", "v"), ("g32", "g")):
                    P.dma(W[n][:, :], S[key][rows, t0:t0 + TB])
                yield
                P.mm(bA[:, 0:TB], w2b[:, rows], xw1T[:, t0:t0 + TB])
                P.mm(bA[:, TB:2 * TB], a2b[:, rows], xa1T[:, t0:t0 + TB])
                P.act(W["sg"][:, :], bA[:, 0:TB], AF.Sigmoid, bias=w0[:, col])
                P.act(W["ic"][:, :], bA[:, TB:2 * TB], AF.Sigmoid, bias=a0[:, col])
                P.act(gate[:, e8, :], W["g32"][:, :], AF.Silu)
                yield
                P.scan(W["css"][:, :], cm[:, :], W["sg"][:, :], 0.0, ALU.mult, ALU.add)
                P.tt(W["d1"][:, :], W["css"][:, :], W["sg"][:, :], ALU.subtract, eng="pool")
                P.tt(c3(W["d2"][:, :]), c3(W["css"][:, :])[:, :, 63:64].to_broadcast([128, CB, 64]),
                     c3(W["css"][:, :]), ALU.subtract)
                P.ts(W["kk"][:, :], W["k32"][:, :], k_k[:, col], ALU.mult)
                P.tt(W["sq"][:, :], W["kk"][:, :], W["kk"][:, :], ALU.mult, eng="pool")
                yield
                P.act(W["gam"][:, :], W["css"][:, :], AF.Exp, scale=-C0)
                P.act(W["gin"][:, :], W["css"][:, :], AF.Exp, scale=C0)
                P.act(W["gpr"][:, :], W["d1"][:, :], AF.Exp, scale=-C0)
                P.act(W["gen"][:, :], W["d2"][:, :], AF.Exp, scale=-C0)
                P.copy(gC[:, e8, :], c3(W["gam"][:, :])[:, :, 63], eng="pool")
                P.mm(bB[:, 0:TB], bones[:, :], W["sq"][:, :])
                P.act(W["lnq"][:, :], bB[:, 0:TB], AF.Ln)
                P.act(W["rn"][:, :], W["lnq"][:, :], AF.Exp, scale=-0.5)
                yield
                P.tt(W["kkn"][:, :], W["kk"][:, :], W["rn"][:, :], ALU.mult)
                P.stt(AR[:, e8, 0, :], W["kkn"][:, :], -1.0, W["gpr"][:, :], ALU.mult, ALU.mult)
                P.tt(W["bb"][:, :], W["kkn"][:, :], W["ic"][:, :], ALU.mult)
                P.tt(Bt[:, e8, :], W["bb"][:, :], W["gin"][:, :], ALU.mult, eng="pool")
                P.tt(W["bg"][:, :], W["bb"][:, :], W["gen"][:, :], ALU.mult)
                P.ts(W["u"][:, :], W["ic"][:, :], -1.0, ALU.add, k_a[:, col], ALU.mult)
                P.stt(W["km"][:, :], W["u"][:, :], 1.0, W["k32"][:, :], ALU.add, ALU.mult)
                P.tt(Kt[:, e8, :], W["km"][:, :], W["gin"][:, :], ALU.mult, eng="pool")
                P.tt(W["kg"][:, :], W["km"][:, :], W["gen"][:, :], ALU.mult, eng="pool")
                P.tt(AR[:, e8, 1, :], W["r32"][:, :], W["gam"][:, :], ALU.mult)
                P.stt(W["rk"][:, :], W["r32"][:, :], r_k[:, col], W["km"][:, :], ALU.mult, ALU.mult)
                P.copy(W["vb"][:, :], W["v32"][:, :], eng="act")
                yield
                P.mm(bG[:, TB:2 * TB], bones[:, :], W["rk"][:, :])
                P.tt(bonus[:, e8, :], bG[:, TB:2 * TB], W["v32"][:, :], ALU.mult)
                for par in range(2):
                    for ch in range(CB):
                        ps_ = slice(par * 64, (par + 1) * 64)
                        for j, nm in enumerate(("bg", "kg", "vb")):
                            o_ = (ch * 3 + j) * 64
                            P.tr(bT[ps_, o_:o_ + 64], W[nm][ps_, ch * 64:(ch + 1) * 64], ident[ps_, ps_])
                P.copy(TM[:, e8, :, :, :].rearrange("p c j t -> p (c j t)"), bT[:, 0:CB * 3 * 64], eng="act")
                yield
                mk3 = suui[:, :, :].rearrange("p a t -> p (a t)").unsqueeze(1).to_broadcast([128, CB, 128])
                for par in range(2):
                    for ch in range(CB):
                        cs_ = slice(ch * 64, (ch + 1) * 64)
                        ps_ = slice(par * 64, (par + 1) * 64)
                        P.mm(bM[ps_, ch * 128:(ch + 1) * 128], Bt[ps_, e8, cs_], AR[ps_, e8, :, cs_])
                P.tt(NTM[:, e8, :, :, :].rearrange("p c a t -> p c (a t)"),
                     bM[:, :].rearrange("p (c x) -> p c x", x=128), mk3, ALU.mult)
                P.copy(NTm[0][:, e8, :, :], NTM[:, e8, :, 0, :], eng="pool")
                P.tt(TTf[:, e8, :, :], NTM[:, e8, :, 0, :], i64[:, :].unsqueeze(1).to_broadcast([128, CB, 64]), ALU.add)
                P.copy(TTb[:, e8, :, :], TTf[:, e8, :, :], eng="act")
                yield
                for par in range(2):
                    for ch in range(CB):
                        cs_ = slice(ch * 64, (ch + 1) * 64)
                        ps_ = slice(par * 64, (par + 1) * 64)
                        P.mm(bM[ps_, ch * 128:(ch + 1) * 128], Kt[ps_, e8, cs_], AR[ps_, e8, :, cs_])
                P.tt(LM[:, e8, :, :, :].rearrange("p c a t -> p c (a t)"),
                     bM[:, :].rearrange("p (c x) -> p c x", x=128), mk3, ALU.mult)
                yield
                for par in range(2):
                    for ch in range(CB):
                        cs_ = slice(ch * 64, (ch + 1) * 64)
                        ps_ = slice(par * 64, (par + 1) * 64)
                        P.mm(bM[ps_, ch * 64:(ch + 1) * 64], AR[ps_, e8, 0, cs_], Bt[ps_, e8, cs_])
                P.tt(Nm[0][:, e8, :, :], bM[:, 0:CB * 64].rearrange("p (c x) -> p c x", x=64),
                     slm[:, :].unsqueeze(1).to_broadcast([128, CB, 64]), ALU.mult)

            def doubling(ecs):
                for m in range(1, DBG.get('lv', 6)):
                    a, b = (m - 1) % 2, m % 2
                    for i, ec in enumerate(ecs):
                        e8 = ec % G8
                        bkT = P.bank[3 + (i % 2) * 2]
                        bkN = P.bank[4 + (i % 2) * 2]
                        for par in range(2):
                            for ch in range(CB):
                                ps_ = slice(par * 64, (par + 1) * 64)
                                P.mm(bkN[ps_, ch * 64:(ch + 1) * 64], NTm[a][ps_, e8, ch, :], Nm[a][ps_, e8, ch, :])
                                if m < 5:
                                    P.mm(bkT[ps_, ch * 64:(ch + 1) * 64], Nm[a][ps_, e8, ch, :], NTm[a][ps_, e8, ch, :])
                        P.copy(Nm[b][:, e8, :, :], bkN[:, 0:256].rearrange("p (c x) -> p c x", x=64), eng="dve")
                        if m < 5:
                            P.copy(NTm[b][:, e8, :, :], bkT[:, 0:256].rearrange("p (c x) -> p c x", x=64), eng=DBG.get("ntm_eng", "act"))
                    for i, ec in enumerate(ecs if DBG.get('tt', 1) else []):
                        e8 = ec % G8
                        bkN = P.bank[4 + (i % 2) * 2]
                        for par in range(2):
                            for ch in range(CB):
                                ps_ = slice(par * 64, (par + 1) * 64)
                                P.mm(bkN[ps_, 256 + ch * 64:256 + (ch + 1) * 64], Nm[b][ps_, e8, ch, :], TTb[ps_, e8, ch, :])
                        P.tt(TTf[:, e8, :, :], bkN[:, 256:512].rearrange("p (c x) -> p c x", x=64), TTf[:, e8, :, :], ALU.add)
                        P.copy(TTb[:, e8, :, :], TTf[:, e8, :, :], eng="act")
                    yield

            def chain(g, blk):
                ecs = list(range(g * G8, (g + 1) * G8))
                bX, bU, bY, bZ = P.bank[0], P.bank[1], P.bank[6], P.bank[7]
                for ch in range(CB):
                    cs_ = slice(ch * 64, (ch + 1) * 64)
                    for par in range(2):
                        for e8, ec in enumerate(ecs):
                            ps_ = slice(par * 64, (par + 1) * 64)
                            o_ = bX[ps_, e8 * 64:(e8 + 1) * 64]
                            P.mm(o_, AR[ps_, e8, 0, cs_], Zb[ps_, ec, :], start=True, stop=False)
                            P.mm(o_, LM[ps_, e8, ch, 0, :], TM[ps_, e8, ch, 2, :], start=False, stop=True)
                    P.copy(Xb[:, :, :], bX[:, :].rearrange("p (e v) -> p e v", v=64), eng="act")
                    for par in range(2):
                        for e8, ec in enumerate(ecs):
                            ps_ = slice(par * 64, (par + 1) * 64)
                            P.mm(bU[ps_, e8 * 64:(e8 + 1) * 64], TTb[ps_, e8, ch, :], Xb[ps_, e8, :])
                    P.copy(Ub[:, :, :], bU[:, :].rearrange("p (e v) -> p e v", v=64), eng="dve")
                    for par in range(2):
                        for e8, ec in enumerate(ecs):
                            ps_ = slice(par * 64, (par + 1) * 64)
                            o_ = bY[ps_, e8 * 64:(e8 + 1) * 64]
                            P.mm(o_, Zb[ps_, ec, :], AR[ps_, e8, 1, cs_], start=True, stop=False)
                            P.mm(o_, Ub[ps_, e8, :], NTM[ps_, e8, ch, 1, :], start=False, stop=False)
                            P.mm(o_, TM[ps_, e8, ch, 2, :], LM[ps_, e8, ch, 1, :], start=False, stop=True)
                            o2 = bZ[ps_, e8 * 64:(e8 + 1) * 64]
                            P.mm(o2, TM[ps_, e8, ch, 0, :], Ub[ps_, e8, :], start=True, stop=False)
                            P.mm(o2, TM[ps_, e8, ch, 1, :], TM[ps_, e8, ch, 2, :], start=False, stop=True)
                    P.copy(Ysb[:, :, cs_], bY[:, :].rearrange("p (e t) -> p e t", t=64), eng="act")
                    zs = Zf[:, g * G8:(g + 1) * G8, :]
                    P.tt(zs, zs, gC[:, :, ch:ch + 1].to_broadcast([128, G8, 64]), ALU.mult)
                    P.tt(zs, bZ[:, :].rearrange("p (e v) -> p e v", v=64), zs, ALU.add)
                    P.copy(Zb[:, g * G8:(g + 1) * G8, :], zs, eng="act")

            def finalize(ec, blk):
                e8 = ec % G8
                t0 = blk * TB
                W = tmps[ec % NSET]
                col = slice(ec, ec + 1)
                bS = P.bank[2 + ec % 2]
                y = Ysb[:, e8, :]
                P.tt(W["y2"][:, :], y, y, ALU.mult, eng="pool")
                yield
                P.mm(bS[:, 0:TB], bo32[:, :], y)
                P.mm(bS[:, TB:2 * TB], bo32[:, :], W["y2"][:, :])
                P.copy(W["mean"][:, :], bS[:, 0:TB], eng="dve")
                P.tt(W["msq"][:, :], W["mean"][:, :], W["mean"][:, :], ALU.mult, eng="pool")
                P.tt(W["var"][:, :], bS[:, TB:2 * TB], W["msq"][:, :], ALU.subtract)
                yield
                P.act(W["sd"][:, :], W["var"][:, :], AF.Sqrt, bias=gne[:, 0:1])
                P.recip(W["rstd"][:, :], W["sd"][:, :])
                yield
                P.tt(W["dd"][:, :], y, W["mean"][:, :], ALU.subtract, eng="pool")
                P.tt(W["dd"][:, :], W["dd"][:, :], W["rstd"][:, :], ALU.mult)
                yield
                P.ts(W["yn"][:, :], W["dd"][:, :], ln_w[:, col], ALU.mult, ln_b[:, col], ALU.add)
                P.tt(W["yn"][:, :], W["yn"][:, :], bonus[:, e8, :], ALU.add, eng="pool")
                P.tt(W["yo"][:, :], W["yn"][:, :], gate[:, e8, :], ALU.mult, eng="pool")
                P.dma(S["y"][ec * 128:(ec + 1) * 128, t0:t0 + TB], W["yo"][:, :])

            for blk in range(DBG.get("nblk", T // TB)):
                for g in range(DBG.get("ng", 16 // G8)):
                    ecs = list(range(g * G8, (g + 1) * G8))
                    for i in range(0, G8, NSET):
                        interleave([prep(ec, blk) for ec in ecs[i:i + NSET]])
                    interleave([doubling(ecs)])
                    chain(g, blk)
                    for i in range(0, G8, NSET):
                        interleave([finalize(ec, blk) for ec in ecs[i:i + NSET]])
    phase_outproj(P, C, S["y"], w_out_d, x_d, x1_d)


FUSED = 1
_CACHE = {}


def _decl_l0(P, x1_kind):
    d = lambda n, sh, kind="ExternalInput", dt_=F32: P.dram(n, sh, dt_, kind=kind).ap()
    a = dict(x_d=d("x", [T, D]), nw_d=d("nw0", [D]), mu_d=d("mu", [6, D]), w_in_d=d("w_in0", [D, 4 * D]),
             w0_d=d("w0", [D]), w1_d=d("w1", [D, 96]), w2_d=d("w2", [96, D]),
             a0_d=d("a0", [D]), a1_d=d("a1", [D, 96]), a2_d=d("a2", [96, D]),
             kk_d=d("k_k", [D]), ka_d=d("k_a", [D]), rk_d=d("r_k", [D]), lnw_d=d("ln_w", [D]), lnb_d=d("ln_b", [D]),
             w_out_d=d("w_out0", [D, D]))
    a["x1_d"] = d("x1", [T, D], kind=x1_kind)
    S = {k: P.dram("s0_" + k, [D, T], F32).ap() for k in ("r", "k", "v", "g")}
    S["y"] = P.dram("s0_y", [D, T], BF16).ap()
    a["S"] = S
    return a


def _decl_l1(P, x1_ap):
    d = lambda n, sh, kind="ExternalInput", dt_=F32: P.dram(n, sh, dt_, kind=kind).ap()
    a = dict(x1_d=x1_ap if x1_ap is not None else d("x1", [T, D]), nw_d=d("nw1", [D]), w_in_d=d("w_in1", [D, 4 * D]),
             lam_d=d("lam", [4, 64]), subln_d=d("subln", [128]), w_out_d=d("w_out1", [D, D]), fnw_d=d("fnw", [D]),
             out_d=d("out", [T, D], kind="ExternalOutput"))
    S = {k: P.dram("s1_" + k, [D, T], BF16).ap() for k in ("q", "k", "y")}
    S["v"] = P.dram("s1_v", [T, D], BF16).ap()
    S["g"] = P.dram("s1_g", [T, D], BF16).ap()
    a["S"] = S
    a["lambda_init"] = 0.8 - 0.6 * math.exp(-0.3 * 1)
    return a


def _build_fused():
    P, C = setup_prog()
    a0 = _decl_l0(P, "Internal")
    layer0(P, C, **a0)
    a1 = _decl_l1(P, a0["x1_d"])
    layer1(P, C, a1["x1_d"], a1["nw_d"], a1["w_in_d"], a1["lam_d"], a1["subln_d"], a1["w_out_d"], a1["fnw_d"],
           a1["out_d"], a1["lambda_init"], a1["S"])
    P.finish([a1["out_d"]])
    return P.build()


def _build_l0():
    P, C = setup_prog()
    a0 = _decl_l0(P, "ExternalOutput")
    layer0(P, C, **a0)
    P.finish([a0["x1_d"]])
    return P.build()


def _build_l1():
    P, C = setup_prog()
    a1 = _decl_l1(P, None)
    layer1(P, C, a1["x1_d"], a1["nw_d"], a1["w_in_d"], a1["lam_d"], a1["subln_d"], a1["w_out_d"], a1["fnw_d"],
           a1["out_d"], a1["lambda_init"], a1["S"])
    P.finish([a1["out_d"]])
    return P.build()


def _get(name, fn):
    if name not in _CACHE:
        _CACHE[name] = fn()
    return _CACHE[name]


def kernel(x, norm_w, rwkv_mu, rwkv_w_in, rwkv_w0, rwkv_w1, rwkv_w2, rwkv_a0, rwkv_a1, rwkv_a2,
           rwkv_k_k, rwkv_k_a, rwkv_r_k, rwkv_ln_w, rwkv_ln_b, rwkv_w_out,
           diff_w_in, diff_lambda, diff_subln_w, diff_w_out, final_norm_w):
    f = lambda v: np.ascontiguousarray(np.asarray(v), dtype=np.float32)
    x = f(x)
    n = x.shape[0]
    m0 = {"nw0": f(norm_w)[0], "mu": f(rwkv_mu)[0], "w_in0": f(rwkv_w_in)[0],
          "w0": f(rwkv_w0)[0], "w1": f(rwkv_w1)[0], "w2": f(rwkv_w2)[0],
          "a0": f(rwkv_a0)[0], "a1": f(rwkv_a1)[0], "a2": f(rwkv_a2)[0],
          "k_k": f(rwkv_k_k)[0], "k_a": f(rwkv_k_a)[0], "r_k": f(rwkv_r_k)[0].reshape(-1),
          "ln_w": f(rwkv_ln_w)[0], "ln_b": f(rwkv_ln_b)[0], "w_out0": f(rwkv_w_out)[0]}
    m1 = {"nw1": f(norm_w)[1], "w_in1": f(diff_w_in)[0], "lam": f(diff_lambda)[0],
          "subln": f(diff_subln_w)[0], "w_out1": f(diff_w_out)[0], "fnw": f(final_norm_w)}
    cores = list(range(n))
    if FUSED:
        nc = _get("fused", _build_fused)
        maps = [dict(m0, **m1, x=x[i]) for i in range(n)]
        res = run_bass_kernel_spmd(nc, maps, core_ids=cores)
        return np.stack([np.asarray(res.results[i]["out"], dtype=np.float32) for i in range(n)], 0)
    nc0 = _get("l0", _build_l0)
    res0 = run_bass_kernel_spmd(nc0, [dict(m0, x=x[i]) for i in range(n)], core_ids=cores)
    nc1 = _get("l1", _build_l1)
    res1 = run_bass_kernel_spmd(nc1, [dict(m1, x1=np.asarray(res0.results[i]["x1"], dtype=np.float32)) for i in range(n)],
                                core_ids=cores)
    return np.stack([np.asarray(res1.results[i]["out"], dtype=np.float32) for i in range(n)], 0)
```
